# Optimizing a Trainium2 kernel written in Bass

```python
import math
import jax, jax.numpy as jnp
from jax import lax
import numpy as np

D_MODEL = 1024
BATCH = 8
SEQ = 2048
DEPTH = 1
DEC_BATCH = 1
DEC_SEQ = 16384
PAST_LEN = 128

HYENA_WIDTH = 512
HYENA_ORDER = 2
SHORT_CONV = 3
FILTER_BANDS = 16
FILTER_EMB = 1 + 2 * FILTER_BANDS
FILTER_HIDDEN = 64
N_DIRECTIONS = 2
DECAY_FAST_PCT = 0.3
DECAY_SLOW_PCT = 1.5
DECAY_TARGET = 1e-2
DECAY_SHIFT = 0.05
N_HEADS = 8
N_KV_HEADS = 2
HEAD_DIM = 64
ATTN_WIDTH = N_HEADS * HEAD_DIM
KV_WIDTH = N_KV_HEADS * HEAD_DIM
WINDOW = 128
BLOCK = 128
ROPE_THETA = 10000.0
N_BRANCHES = 2
IN_WIDTH = 3 * HYENA_WIDTH + ATTN_WIDTH + 2 * KV_WIDTH + N_BRANCHES * D_MODEL
FFN_HIDDEN = -(-8 * D_MODEL // (3 * 256)) * 256
RMS_EPS = 1e-6
NEG_INF = -1e30

kernel_name = "hybrid_hyena_swa_sink_encoder"


def rmsnorm(x, w):
    xf = x.astype(jnp.float32)
    y = xf * lax.rsqrt(jnp.mean(xf * xf, axis=-1, keepdims=True) + RMS_EPS) * w.astype(jnp.float32)
    return y.astype(x.dtype)


def short_conv3(u, w, b):
    up = jnp.pad(u, ((0, 0), (1, 1), (0, 0)))
    return w[0] * up[:, :-2] + w[1] * up[:, 1:-1] + w[2] * up[:, 2:] + b


def hyena_filter_spectrum(L, w1, b1, w2, b2, w3, freq):
    f32 = jnp.float32
    pos = jnp.arange(L, dtype=f32)
    t = pos / (L - 1)
    bands = jnp.linspace(1e-4, FILTER_BANDS - 1, FILTER_BANDS, dtype=f32)
    ang = (2.0 * math.pi / L) * pos[:, None] * bands[None, :]
    z = jnp.concatenate([t[:, None], jnp.cos(ang), -jnp.sin(ang)], axis=-1)
    fr = freq.astype(f32)
    h = jnp.sin(fr * (z @ w1.astype(f32) + b1.astype(f32)))
    h = jnp.sin(fr * (h @ w2.astype(f32) + b2.astype(f32)))
    h = (h @ w3.astype(f32)).reshape(L, HYENA_ORDER, N_DIRECTIONS, HYENA_WIDTH)
    max_decay = math.log(DECAY_TARGET) / DECAY_FAST_PCT
    min_decay = math.log(DECAY_TARGET) / DECAY_SLOW_PCT
    deltas = jnp.linspace(min_decay, max_decay, HYENA_WIDTH, dtype=f32)
    window = jnp.exp(-t[:, None] * jnp.abs(deltas)[None, :]) + DECAY_SHIFT
    h = h * window[:, None, None, :]
    fwd, bwd = h[:, :, 0], h[:, :, 1]
    k = jnp.concatenate([fwd, jnp.zeros((1, HYENA_ORDER, HYENA_WIDTH), f32), jnp.flip(bwd[1:], axis=0)], axis=0)
    k = k / jnp.sum(jnp.abs(k), axis=0, keepdims=True)
    return jnp.fft.rfft(k, axis=0)


def fft_long_conv(u, spec):
    L = u.shape[1]
    U = jnp.fft.rfft(u, n=2 * L, axis=1)
    return jnp.fft.irfft(U * spec[None], n=2 * L, axis=1)[:, :L]


def hyena_mixer(u, conv_w, conv_b, spec, skip_bias):
    uc = short_conv3(u.astype(jnp.float32), conv_w.astype(jnp.float32), conv_b.astype(jnp.float32))
    x1, x2, v = jnp.split(uc, 3, axis=-1)
    sb = skip_bias.astype(jnp.float32)
    z = v
    for o, gate in enumerate((x1, x2)):
        z = gate * (fft_long_conv(z, spec[:, o]) + sb[o] * z)
    return z.astype(u.dtype)


def apply_rope(x):
    L = x.shape[1]
    inv = ROPE_THETA ** (-jnp.arange(0, HEAD_DIM, 2, dtype=jnp.float32) / HEAD_DIM)
    ang = jnp.arange(L, dtype=jnp.float32)[:, None] * inv[None, :]
    cos = jnp.cos(ang)[None, :, None, :]
    sin = jnp.sin(ang)[None, :, None, :]
    xf = x.astype(jnp.float32)
    a, b = xf[..., :HEAD_DIM // 2], xf[..., HEAD_DIM // 2:]
    return jnp.concatenate([a * cos - b * sin, b * cos + a * sin], axis=-1).astype(x.dtype)


def windowed_attention(q, k, v, sink):
    B, L = q.shape[:2]
    nb = L // BLOCK
    G = N_HEADS // N_KV_HEADS
    qb = q.reshape(B, nb, BLOCK, N_KV_HEADS, G, HEAD_DIM)

    def band(t):
        tp = jnp.pad(t, ((0, 0), (BLOCK, BLOCK), (0, 0), (0, 0))).reshape(B, nb + 2, BLOCK, N_KV_HEADS, HEAD_DIM)
        return jnp.concatenate([tp[:, :-2], tp[:, 1:-1], tp[:, 2:]], axis=2)

    kb, vb = band(k), band(v)
    s = jnp.einsum('bnqkgd,bnskd->bnkgqs', qb, kb, preferred_element_type=jnp.float32) * (HEAD_DIM ** -0.5)
    qi = jnp.arange(BLOCK)[:, None]
    si = jnp.arange(3 * BLOCK)[None, :]
    rel = si - BLOCK - qi
    kpos = (jnp.arange(nb)[:, None, None] - 1) * BLOCK + si[None]
    mask = (jnp.abs(rel) <= WINDOW)[None] & (kpos >= 0) & (kpos < L)
    s = jnp.where(mask[None, :, None, None], s, NEG_INF)
    sink_logit = jnp.broadcast_to(sink.astype(jnp.float32).reshape(1, 1, N_KV_HEADS, G, 1, 1), s.shape[:-1] + (1,))
    p = jax.nn.softmax(jnp.concatenate([s, sink_logit], axis=-1), axis=-1)[..., :-1]
    o = jnp.einsum('bnkgqs,bnskd->bnqkgd', p.astype(v.dtype), vb)
    return o.reshape(B, L, ATTN_WIDTH)


def encoder_layer(x, attn_norm_w, w_in, hyena_conv_w, hyena_conv_b, filt_w1, filt_b1, filt_w2, filt_b2,
                  filt_w3, filt_freq, hyena_bias, q_norm_w, k_norm_w, attn_sink, w_hy_out, w_at_out, w_o,
                  ffn_norm_w, w_gate, w_up, w_down):
    B, L, _ = x.shape
    h = rmsnorm(x, attn_norm_w)
    proj = h @ w_in
    c0 = 3 * HYENA_WIDTH
    c1 = c0 + ATTN_WIDTH
    c2 = c1 + KV_WIDTH
    c3 = c2 + KV_WIDTH
    u_hy = proj[..., :c0]
    q = proj[..., c0:c1].reshape(B, L, N_HEADS, HEAD_DIM)
    k = proj[..., c1:c2].reshape(B, L, N_KV_HEADS, HEAD_DIM)
    v = proj[..., c2:c3].reshape(B, L, N_KV_HEADS, HEAD_DIM)
    g_hy = proj[..., c3:c3 + D_MODEL]
    g_at = proj[..., c3 + D_MODEL:]
    spec = hyena_filter_spectrum(L, filt_w1, filt_b1, filt_w2, filt_b2, filt_w3, filt_freq)
    y_hy = hyena_mixer(u_hy, hyena_conv_w, hyena_conv_b, spec, hyena_bias)
    q = apply_rope(rmsnorm(q, q_norm_w))
    k = apply_rope(rmsnorm(k, k_norm_w))
    y_at = windowed_attention(q, k, v, attn_sink)
    merged = jax.nn.sigmoid(g_hy) * (y_hy @ w_hy_out) + jax.nn.sigmoid(g_at) * (y_at @ w_at_out)
    x = x + merged @ w_o
    f = rmsnorm(x, ffn_norm_w)
    x = x + (jax.nn.silu(f @ w_gate) * (f @ w_up)) @ w_down
    return x


def trunk(x, attn_norm_w, w_in, hyena_conv_w, hyena_conv_b, filt_w1, filt_b1, filt_w2, filt_b2, filt_w3,
          filt_freq, hyena_bias, q_norm_w, k_norm_w, attn_sink, w_hy_out, w_at_out, w_o, ffn_norm_w,
          w_gate, w_up, w_down):
    for l in range(DEPTH):
        x = encoder_layer(x, attn_norm_w[l], w_in[l], hyena_conv_w[l], hyena_conv_b[l], filt_w1[l], filt_b1[l],
                          filt_w2[l], filt_b2[l], filt_w3[l], filt_freq[l], hyena_bias[l], q_norm_w[l],
                          k_norm_w[l], attn_sink[l], w_hy_out[l], w_at_out[l], w_o[l], ffn_norm_w[l],
                          w_gate[l], w_up[l], w_down[l])
    return x


def setup_inputs(seed: int = 0) -> dict:
    key = jax.random.key(seed)
    ks = jax.random.split(key, 24)
    f32 = jnp.float32

    def nrm(k, shape, scale):
        return jax.random.normal(k, shape, f32) * scale

    return {
        'x_prompt': nrm(ks[0], (BATCH, SEQ, D_MODEL), 1.0),
        'x_sample': nrm(ks[1], (DEC_BATCH, DEC_SEQ, D_MODEL), 1.0),
        'attn_norm_w': 1.0 + nrm(ks[2], (DEPTH, D_MODEL), 0.01),
        'w_in': nrm(ks[3], (DEPTH, D_MODEL, IN_WIDTH), D_MODEL ** -0.5),
        'hyena_conv_w': nrm(ks[4], (DEPTH, SHORT_CONV, 3 * HYENA_WIDTH), SHORT_CONV ** -0.5),
        'hyena_conv_b': nrm(ks[5], (DEPTH, 3 * HYENA_WIDTH), 0.01),
        'filt_w1': nrm(ks[6], (DEPTH, FILTER_EMB, FILTER_HIDDEN), FILTER_EMB ** -0.5),
        'filt_b1': nrm(ks[7], (DEPTH, FILTER_HIDDEN), 0.1),
        'filt_w2': nrm(ks[8], (DEPTH, FILTER_HIDDEN, FILTER_HIDDEN), FILTER_HIDDEN ** -0.5),
        'filt_b2': nrm(ks[9], (DEPTH, FILTER_HIDDEN), 0.1),
        'filt_w3': nrm(ks[10], (DEPTH, FILTER_HIDDEN, HYENA_ORDER * N_DIRECTIONS * HYENA_WIDTH), FILTER_HIDDEN ** -0.5),
        'filt_freq': 1.0 + nrm(ks[11], (DEPTH, FILTER_HIDDEN), 0.01),
        'hyena_bias': nrm(ks[12], (DEPTH, HYENA_ORDER, HYENA_WIDTH), 1.0),
        'q_norm_w': 1.0 + nrm(ks[13], (DEPTH, HEAD_DIM), 0.01),
        'k_norm_w': 1.0 + nrm(ks[14], (DEPTH, HEAD_DIM), 0.01),
        'attn_sink': nrm(ks[15], (DEPTH, N_HEADS), 1.0),
        'w_hy_out': nrm(ks[16], (DEPTH, HYENA_WIDTH, D_MODEL), HYENA_WIDTH ** -0.5),
        'w_at_out': nrm(ks[17], (DEPTH, ATTN_WIDTH, D_MODEL), ATTN_WIDTH ** -0.5),
        'w_o': nrm(ks[18], (DEPTH, D_MODEL, D_MODEL), D_MODEL ** -0.5),
        'ffn_norm_w': 1.0 + nrm(ks[19], (DEPTH, D_MODEL), 0.01),
        'w_gate': nrm(ks[20], (DEPTH, D_MODEL, FFN_HIDDEN), D_MODEL ** -0.5),
        'w_up': nrm(ks[21], (DEPTH, D_MODEL, FFN_HIDDEN), D_MODEL ** -0.5),
        'w_down': nrm(ks[22], (DEPTH, FFN_HIDDEN, D_MODEL), FFN_HIDDEN ** -0.5),
    }


def reference(x_prompt, x_sample, attn_norm_w, w_in, hyena_conv_w, hyena_conv_b, filt_w1, filt_b1, filt_w2,
              filt_b2, filt_w3, filt_freq, hyena_bias, q_norm_w, k_norm_w, attn_sink, w_hy_out, w_at_out, w_o,
              ffn_norm_w, w_gate, w_up, w_down):
    y_prompt = trunk(x_prompt, attn_norm_w, w_in, hyena_conv_w, hyena_conv_b, filt_w1, filt_b1, filt_w2, filt_b2,
                     filt_w3, filt_freq, hyena_bias, q_norm_w, k_norm_w, attn_sink, w_hy_out, w_at_out, w_o,
                     ffn_norm_w, w_gate, w_up, w_down)
    y_sample = trunk(x_sample, attn_norm_w, w_in, hyena_conv_w, hyena_conv_b, filt_w1, filt_b1, filt_w2, filt_b2,
                     filt_w3, filt_freq, hyena_bias, q_norm_w, k_norm_w, attn_sink, w_hy_out, w_at_out, w_o,
                     ffn_norm_w, w_gate, w_up, w_down)
    return (y_prompt, y_sample)
```

```python
import math
import numpy as np
import concourse.bass as bass
import concourse.mybir as mybir
from concourse.bass_utils import run_bass_kernel_spmd
from contextlib import ExitStack

F32 = mybir.dt.float32
BF16 = mybir.dt.bfloat16
ALU = mybir.AluOpType
AF = mybir.ActivationFunctionType
AX = mybir.AxisListType

D = 1024
HW = 512
NTOK = 2048
INW = 4352
FFN = 2816
EPS = 1e-6
NCORES = 8
ENABLE_HYENA = False
import os
KSTOP = int(os.environ.get('KSTOP', '9'))
KGROUPS = int(os.environ.get('KGROUPS', '2'))
KATT = int(os.environ.get('KATT', '9'))
KHY = int(os.environ.get('KHY', '3'))
KCORE_T10 = 0
RELAXED = tuple(os.environ.get('KRELAX', 'pe').split(','))


class Buf:
    __slots__ = ("w", "r")

    def __init__(self):
        self.w = None
        self.r = {}


class FW:
    def __init__(self, nc, sems, dma_sems):
        self.nc = nc
        self.eng = {"pe": nc.tensor, "act": nc.scalar, "dve": nc.vector, "pool": nc.gpsimd, "sp": nc.sync}
        self.sem = sems
        self.dsem = dma_sems
        self.dcnt = [0] * len(dma_sems)
        self.dnext = 0
        self.dnext_sw = 0
        self.cnt = {e: 0 for e in sems}
        self.ops = {e: [] for e in self.eng}
        self.known = {e: {} for e in self.eng}

    def _deps(self, reads, writes):
        deps = {}

        def add(k, v):
            if deps.get(k, 0) < v:
                deps[k] = v
        for b in reads:
            if b.w is not None:
                add(*b.w)
        for b in writes:
            if b.w is not None:
                add(*b.w)
            for k, v in b.r.items():
                add(k, v)
        return deps

    def _semobj(self, k):
        return self.sem[k] if isinstance(k, str) else self.dsem[k]

    def _wait(self, e, deps, skip_pe=True):
        for k, v in deps.items():
            if skip_pe and k == e and e in RELAXED:
                continue
            if self.known[e].get(k, 0) >= v:
                continue
            self.known[e][k] = v
            so = self._semobj(k)
            eo = self.eng[e]
            self.ops[e].append(lambda eo=eo, so=so, v=v: eo.wait_ge(so, v))

    def op(self, e, fn, reads=(), writes=()):
        self._wait(e, self._deps(reads, writes))
        self.cnt[e] += 1
        tok = (e, self.cnt[e])
        so = self.sem[e]
        self.ops[e].append(lambda fn=fn, so=so: fn().then_inc(so, 1))
        for b in writes:
            b.w = tok
            b.r = {}
        for b in reads:
            if b.r.get(e, 0) < tok[1]:
                b.r[e] = tok[1]

    def dma(self, q, fn, reads=(), writes=()):
        self._wait(q, self._deps(reads, writes))
        if q == "pool":
            i = self.dnext_sw
            self.dnext_sw = (self.dnext_sw + 1) % 8
        else:
            i = 8 + self.dnext
            self.dnext = (self.dnext + 1) % (len(self.dsem) - 8)
        if self.dcnt[i] > 0:
            self._wait(q, {i: self.dcnt[i]})
        self.dcnt[i] += 16
        tok = (i, self.dcnt[i])
        so = self.dsem[i]
        self.ops[q].append(lambda fn=fn, so=so: fn().then_inc(so, 16))
        for b in writes:
            b.w = tok
            b.r = {}
        for b in reads:
            if b.r.get(i, 0) < tok[1]:
                b.r[i] = tok[1]

    def barrier(self):
        deps = {k: v for k, v in self.cnt.items() if v > 0}
        for i, v in enumerate(self.dcnt):
            if v > 0:
                deps[i] = v
        for e in self.eng:
            self._wait(e, deps, skip_pe=False)

    def replay(self, block):
        fw = self

        @block.tensor
        def _(e):
            for f in fw.ops["pe"]:
                f()

        @block.scalar
        def _(e):
            for f in fw.ops["act"]:
                f()

        @block.vector
        def _(e):
            for f in fw.ops["dve"]:
                f()

        @block.gpsimd
        def _(e):
            for f in fw.ops["pool"]:
                f()

        @block.sync
        def _(e):
            for f in fw.ops["sp"]:
                f()


class Rot:
    def __init__(self, n, mk):
        self.items = [(mk(), Buf()) for _ in range(n)]
        self.i = 0

    def next(self):
        it = self.items[self.i % len(self.items)]
        self.i += 1
        return it


class Arena:
    def __init__(self, ap, ncols):
        self.ap = ap
        self.n = ncols
        self.off = 0

    def alloc(self, cols, dt=BF16):
        nb = cols * (2 if dt == F32 else 1)
        nb = (nb + 31) // 32 * 32
        assert self.off + nb <= self.n, ("arena overflow", self.off, nb, self.n)
        v = self.ap[:, self.off:self.off + nb]
        self.off += nb
        if dt == F32:
            v = v.bitcast(F32)[:, 0:cols]
        else:
            v = v[:, 0:cols]
        return v

    def mark(self):
        return self.off

    def reset(self, m):
        self.off = m


def build_program():
    nc = bass.Bass("TRN2", target_bir_lowering=False)

    def din(name, shape):
        return nc.dram_tensor(name, list(shape), F32, kind="ExternalInput").ap()

    xp = din("xp", [NTOK, D])
    xs = din("xs", [NTOK + 256, D])
    w_in = din("w_in", [D, INW])
    w_hy = din("w_hy", [HW, D])
    w_at = din("w_at", [HW, D])
    w_o = din("w_o", [D, D])
    w_gate = din("w_gate", [D, FFN])
    w_up = din("w_up", [D, FFN])
    w_down = din("w_down", [FFN, D])
    g1_bc = din("g1_bc", [128, D])
    g2_bc = din("g2_bc", [128, D])
    qw_bc = din("qw_bc", [128, 512])
    kw_bc = din("kw_bc", [128, 128])
    sink_bc = din("sink_bc", [128, 8])
    rope_p = din("rope_p", [128, 16, 64])
    rope_s = din("rope_s", [128, 18, 64])
    masks = din("masks", [128, 4, 128])
    ident_d = din("ident", [128, 128])
    xs_all = din("xs_all", [16384, D])
    w_in_hy = din("w_in_hy", [D, 1536])
    cw_all = din("cw_all", [1536, 4])
    fw1 = din("fw1", [33, 64])
    fw2 = din("fw2", [64, 64])
    fw3 = din("fw3", [64, 2048])
    fb = din("fb", [64, 3])
    hb_bc = din("hb_bc", [128, 1024])
    absd_bc = din("absd_bc", [128, 512])
    HC = {}
    for tag_, (L_, cg_) in (("p", (2048, 8)), ("s", (16384, 1))):
        R_ = L_ // 128
        NI_ = 2 * cg_ * (R_ + 1)
        HC[tag_] = dict(Gf=din("Gf_" + tag_, [128, 256]), S2=din("S2_" + tag_, [128, 12, 128]), INV=din("INV_" + tag_, [128, 4, NI_]),
                        S1I=din("S1I_" + tag_, [128, 4, 128]), zf=din("zf_" + tag_, [2, 33, L_]), negt=din("negt_" + tag_, [128, 2, R_]))
    HC["s"]["INVo"] = din("INVo_s", [128, 4, 34])
    HC["s"]["Atab"] = din("Atab_s", [2, 512 * 128])
    HC["s"]["Btab"] = din("Btab_s", [128, 2, 512])
    yp = nc.dram_tensor("yp", [NTOK, D], F32, kind="ExternalOutput").ap()
    ys = nc.dram_tensor("ys", [NTOK, D], F32, kind="ExternalOutput").ap()

    with ExitStack() as st:
        E = st.enter_context
        arena_t = E(nc.sbuf_tensor("arena", [128, 106368], BF16))
        psb = [E(nc.psum_tensor("ps%d" % i, [128, 512], F32)) for i in range(8)]
        sems = {k: E(nc.semaphore("s_" + k)) for k in ("pe", "act", "dve", "pool")}
        dsems = [E(nc.semaphore("d%d" % i)) for i in range(40)]
        block = E(nc.Block())
        fw = FW(nc, sems, dsems)
        ar = Arena(arena_t[:, :], 106368)
        V = nc.vector
        A = nc.scalar
        P = nc.tensor
        G = nc.gpsimd

        psbuf = [Buf() for _ in range(8)]
        psi = [0]

        def ps():
            i = psi[0]
            psi[0] = (i + 1) % 8
            return psb[i][:, :], psbuf[i]

        ident = ar.alloc(128)
        b_const = Buf()
        g1 = ar.alloc(D, F32)
        g2 = ar.alloc(D, F32)
        qw = ar.alloc(512, F32)
        kw = ar.alloc(128, F32)
        esink = ar.alloc(8, F32)
        ropep = ar.alloc(16 * 64, F32)
        ropes = ar.alloc(18 * 64, F32)
        msk = ar.alloc(4 * 128)
        ones_bf = ar.alloc(128)
        epsb = ar.alloc(1, F32)
        for dst, src in ((ident, ident_d), (msk, masks.rearrange("p a b -> p (a b)"))):
            fw.dma("pool", lambda dst=dst, src=src: G.dma_start(out=dst, in_=src), writes=[b_const])
        for dst, src in ((g1, g1_bc), (g2, g2_bc), (qw, qw_bc), (kw, kw_bc), (esink, sink_bc),
                         (ropep, rope_p.rearrange("p a b -> p (a b)")), (ropes, rope_s.rearrange("p a b -> p (a b)"))):
            fw.dma("sp", lambda dst=dst, src=src: nc.sync.dma_start(out=dst, in_=src), writes=[b_const])
        fw.op("dve", lambda: V.memset(ones_bf, 1.0), writes=[b_const])
        fw.op("dve", lambda: V.memset(epsb, EPS), writes=[b_const])
        fw.op("act", lambda: A.activation(out=esink, in_=esink, func=AF.Exp), reads=[b_const], writes=[b_const])
        fw.barrier()

        NWB = 3
        wbuf = []
        wbb = [Buf() for _ in range(NWB)]
        wbi = [0]

        def wload(src3):
            i = wbi[0]
            wbi[0] = (i + 1) % NWB
            kch, ncols = src3.shape[1], src3.shape[2]
            dst = wbuf[i][:, 0:kch * ncols].rearrange("p (k c) -> p k c", k=kch)
            fw.dma("pool", lambda dst=dst, src3=src3: G.dma_start(out=dst, in_=src3), writes=[wbb[i]])
            return dst, wbb[i]

        def wload_multi(srcs):
            i = wbi[0]
            wbi[0] = (i + 1) % NWB
            kch = srcs[0].shape[1]
            tot = sum(x.shape[2] for x in srcs)
            dst = wbuf[i][:, 0:kch * tot].rearrange("p (k c) -> p k c", k=kch)
            off = 0
            for src3 in srcs:
                n_ = src3.shape[2]
                dv = dst[:, :, off:off + n_]
                fw.dma("pool", lambda dv=dv, src3=src3: G.dma_start(out=dv, in_=src3), writes=[wbb[i]])
                off += n_
            return dst, wbb[i]

        def wsrc(w, kch, c0, ncols, k0=0):
            return w.rearrange("(k p) c -> p k c", p=128)[:, k0:k0 + kch, c0:c0 + ncols]

        xt = [ar.alloc(D, F32) for _ in range(2)]
        xtb = [Buf() for _ in range(2)]
        junkR = Rot(2, lambda: ar.alloc(D))
        hbR = Rot(2, lambda: ar.alloc(D))
        smallR = Rot(4, lambda: ar.alloc(16, F32))

        def rstd_of(src_ap, ncol, groups, reads):
            junk, bjunk = junkR.next()
            sm_, bsmall = smallR.next()
            out_ap = sm_[:, 0:groups]
            jv = junk[:, 0:groups * ncol]
            fw.op("act", lambda: A.activation(out=jv, in_=src_ap, func=AF.Square), reads=reads, writes=[bjunk])
            fw.op("dve", lambda: V.tensor_reduce(out=out_ap, in_=jv.rearrange("p (g c) -> p g c", g=groups),
                                                 axis=AX.X, op=ALU.add), reads=[bjunk], writes=[bsmall])
            fw.op("act", lambda: A.activation(out=out_ap, in_=out_ap, func=AF.Sqrt, bias=epsb, scale=1.0 / ncol),
                  reads=[bsmall, b_const], writes=[bsmall])
            fw.op("dve", lambda: V.reciprocal(out=out_ap, in_=out_ap), reads=[bsmall], writes=[bsmall])
            return out_ap, bsmall

        def norm_transpose(src_fn, tiles, hT, bhT, gbc):
            for n, (ts, td) in enumerate(tiles):
                x_t, bx = xt[n % 2], xtb[n % 2]
                fw.dma("sp", lambda x_t=x_t, ts=ts: nc.sync.dma_start(out=x_t, in_=src_fn(ts)), writes=[bx])
                rs, bsmall = rstd_of(x_t, D, 1, [bx])
                hb, bhb = hbR.next()
                fw.op("dve", lambda x_t=x_t, rs=rs, hb=hb: V.scalar_tensor_tensor(out=hb, in0=x_t, scalar=rs, in1=gbc,
                                                                                 op0=ALU.mult, op1=ALU.mult),
                      reads=[bx, bsmall, b_const], writes=[bhb])
                p, bp = ps()
                pv = p.bitcast(BF16)
                for k in range(8):
                    fw.op("pe", lambda pv=pv, k=k, hb=hb: P.transpose(out=pv[:, k * 128:(k + 1) * 128],
                                                                        in_=hb[:, k * 128:(k + 1) * 128], identity=ident),
                          reads=[bhb, b_const], writes=[bp])
                fw.op("act", lambda pv=pv, td=td: A.copy(out=hT[:, :, td * 128:(td + 1) * 128],
                                                          in_=pv.rearrange("p (k t) -> p k t", k=8)),
                      reads=[bp], writes=[bhT])

        def run_group(xsrc_fn, tiles_in, own0, ntl, rope, ydst, halo_masks, hyT, bhyT):
            m0 = ar.mark()
            NT_ALL = tiles_in * 128
            hT = ar.alloc(8 * NT_ALL).rearrange("p (k t) -> p k t", k=8)
            bhT = Buf()
            norm_transpose(xsrc_fn, [(t, t) for t in range(tiles_in)], hT, bhT, g1)

            if KSTOP <= 1:
                fw.barrier(); ar.reset(m0); return
            yat = ar.alloc(4 * NTOK).rearrange("p (j t) -> p j t", j=4)
            byat = Buf()
            m1 = ar.mark()
            qT = ar.alloc(4 * NTOK).rearrange("p (j t) -> p j t", j=4)
            kT = [[ar.alloc(NT_ALL) for _ in range(2)] for _ in range(2)]
            v2 = ar.alloc(tiles_in * 256).rearrange("p (t c) -> p t c", t=tiles_in)
            bqT, bkT, bv2 = Buf(), Buf(), Buf()
            qfR = Rot(2, lambda: ar.alloc(768, F32))
            qnR = Rot(2, lambda: ar.alloc(640, F32))
            qrR = Rot(2, lambda: ar.alloc(1024))
            for (qr_, bqr_) in qrR.items:
                fw.op("dve", lambda qr_=qr_: V.memset(qr_, 0.0), writes=[bqr_])
            t1R = Rot(2, lambda: ar.alloc(320, F32))
            t2R = Rot(2, lambda: ar.alloc(320, F32))
            wq1, bwq1 = wload(wsrc(w_in, 8, 1536, 512))
            wq2, bwq2 = wload(wsrc(w_in, 8, 2048, 256))
            def qkv_tile(t, qf, bqf, qn, bqn, qr, bqr, t1, bt1, t2, bt2):
                own = own0 <= t < own0 + ntl
                pa, bpa = ps()
                pb, bpb = ps()
                if own:
                    for k in range(8):
                        fw.op("pe", lambda pa=pa, k=k, t=t: P.matmul(pa, lhsT=hT[:, k, t * 128:(t + 1) * 128], rhs=wq1[:, k, :],
                                                                    start=(k == 0), stop=(k == 7)),
                              reads=[bhT, bwq1], writes=[bpa])
                for k in range(8):
                    fw.op("pe", lambda pb=pb, k=k, t=t: P.matmul(pb[:, 0:256], lhsT=hT[:, k, t * 128:(t + 1) * 128], rhs=wq2[:, k, :],
                                                                start=(k == 0), stop=(k == 7)),
                          reads=[bhT, bwq2], writes=[bpb])
                if own:
                    fw.op("act", lambda pa=pa: A.copy(out=qf[:, 0:512], in_=pa), reads=[bpa], writes=[bqf])
                fw.op("act", lambda pb=pb: A.copy(out=qf[:, 512:768], in_=pb[:, 0:256]), reads=[bpb], writes=[bqf])
                for g in range(2):
                    for dup in range(2):
                        fw.op("pool", lambda t=t, g=g, dup=dup: G.tensor_copy(out=v2[:, t, g * 128 + dup * 64:g * 128 + dup * 64 + 64],
                                                                             in_=qf[:, 640 + g * 64:640 + g * 64 + 64]),
                              reads=[bqf], writes=[bv2])
                c0 = 0 if own else 512
                nh = 10 if own else 2
                h0 = c0 // 64
                src = qf[:, c0:640]
                rs, bsmall = rstd_of(src, 64, nh, [bqf])
                qnv = qn[:, c0:640].rearrange("p (h c) -> p h c", c=64)
                fw.op("dve", lambda src=src, rs=rs, nh=nh, qnv=qnv: V.tensor_tensor(
                    out=qnv, in0=src.rearrange("p (h c) -> p h c", c=64),
                    in1=rs.unsqueeze(2).to_broadcast([128, nh, 64]), op=ALU.mult),
                    reads=[bqf, bsmall], writes=[bqn])
                if own:
                    fw.op("dve", lambda: V.tensor_tensor(out=qn[:, 0:512], in0=qn[:, 0:512], in1=qw, op=ALU.mult),
                          reads=[bqn, b_const], writes=[bqn])
                fw.op("dve", lambda: V.tensor_tensor(out=qn[:, 512:640], in0=qn[:, 512:640], in1=kw, op=ALU.mult),
                      reads=[bqn, b_const], writes=[bqn])
                cs = rope[:, t * 64:t * 64 + 32].unsqueeze(1).to_broadcast([128, nh, 32])
                sn = rope[:, t * 64 + 32:t * 64 + 64].unsqueeze(1).to_broadcast([128, nh, 32])
                a_ = qnv[:, :, 0:32]
                b_ = qnv[:, :, 32:64]
                t1v = t1[:, 0:nh * 32].rearrange("p (h c) -> p h c", c=32)
                t2v = t2[:, 0:nh * 32].rearrange("p (h c) -> p h c", c=32)
                if own:
                    dq = qr[:, 0:512].rearrange("p (h c) -> p h c", c=64)
                fw.op("dve", lambda a_=a_, cs=cs, t1v=t1v: V.tensor_tensor(out=t1v, in0=a_, in1=cs, op=ALU.mult),
                      reads=[bqn, b_const], writes=[bt1])
                fw.op("pool", lambda b_=b_, sn=sn, t2v=t2v: G.tensor_tensor(out=t2v, in0=b_, in1=sn, op=ALU.mult),
                      reads=[bqn, b_const], writes=[bt2])
                nq = nh - 2
                if own:
                    fw.op("dve", lambda t1v=t1v, t2v=t2v, dq=dq, nq=nq: V.tensor_tensor(out=dq[:, :, 0:32], in0=t1v[:, 0:nq, :], in1=t2v[:, 0:nq, :], op=ALU.subtract),
                          reads=[bt1, bt2], writes=[bqr])
                for g in range(2):
                    for dup in range(2):
                        o0 = 512 + g * 256 + dup * 192
                        fw.op("dve", lambda t1v=t1v, t2v=t2v, o0=o0, g=g, nq=nq: V.tensor_tensor(
                            out=qr[:, o0:o0 + 32], in0=t1v[:, nq + g, :], in1=t2v[:, nq + g, :], op=ALU.subtract),
                            reads=[bt1, bt2], writes=[bqr])
                fw.op("dve", lambda b_=b_, cs=cs, t1v=t1v: V.tensor_tensor(out=t1v, in0=b_, in1=cs, op=ALU.mult),
                      reads=[bqn, b_const, bqr], writes=[bt1])
                fw.op("pool", lambda a_=a_, sn=sn, t2v=t2v: G.tensor_tensor(out=t2v, in0=a_, in1=sn, op=ALU.mult),
                      reads=[bqn, b_const, bqr], writes=[bt2])
                if own:
                    fw.op("dve", lambda t1v=t1v, t2v=t2v, dq=dq, nq=nq: V.tensor_tensor(out=dq[:, :, 32:64], in0=t1v[:, 0:nq, :], in1=t2v[:, 0:nq, :], op=ALU.add),
                          reads=[bt1, bt2], writes=[bqr])
                for g in range(2):
                    for dup in range(2):
                        o0 = 512 + g * 256 + dup * 192 + 32
                        fw.op("dve", lambda t1v=t1v, t2v=t2v, o0=o0, g=g, nq=nq: V.tensor_tensor(
                            out=qr[:, o0:o0 + 32], in0=t1v[:, nq + g, :], in1=t2v[:, nq + g, :], op=ALU.add),
                            reads=[bt1, bt2], writes=[bqr])
                p, bp = ps()
                pv = p.bitcast(BF16)
                chunks = list(range(4, 8)) + (list(range(4)) if own else [])
                for c in chunks:
                    fw.op("pe", lambda pv=pv, c=c: P.transpose(out=pv[:, c * 128:(c + 1) * 128], in_=qr[:, c * 128:(c + 1) * 128], identity=ident),
                          reads=[bqr, b_const], writes=[bp])
                if own:
                    to = t - own0
                    fw.op("act", lambda pv=pv, to=to: A.copy(out=qT[:, :, to * 128:(to + 1) * 128],
                                                              in_=pv[:, 0:512].rearrange("p (j t) -> p j t", j=4)),
                          reads=[bp], writes=[bqT])
                for g in range(2):
                    for par in range(2):
                        cc = 4 + 2 * g + par
                        fw.op("act", lambda pv=pv, g=g, par=par, t=t, cc=cc: A.copy(out=kT[g][par][:, t * 128:(t + 1) * 128], in_=pv[:, cc * 128:(cc + 1) * 128]),
                              reads=[bp], writes=[bkT])

            for t in range(tiles_in):
                (qf, bqf), (qn, bqn), (qr, bqr), (t1, bt1), (t2, bt2) = qfR.next(), qnR.next(), qrR.next(), t1R.next(), t2R.next()
                qkv_tile(t, qf, bqf, qn, bqn, qr, bqr, t1, bt1, t2, bt2)
            if KSTOP <= 2:
                fw.barrier(); ar.reset(m0); return
            pT = [ar.alloc(512) for _ in range(3)]
            bpT = [Buf() for _ in range(3)]
            rec = ar.alloc(512, F32)
            brec = Buf()
            pidx = 0
            for n in range(ntl):
                tq = own0 + n
                for g in range(2):
                    kbs = [kb for kb in (tq - 1, tq, tq + 1) if 0 <= kb < tiles_in]
                    po, bpo = ps()
                    pd, bpd = ps()
                    for ii, kb in enumerate(kbs):
                        pS, bpS = ps()
                        for hh in range(4):
                            h = 4 * g + hh
                            pbase = (h % 2) * 64
                            j = h // 2
                            fw.op("pe", lambda pS=pS, hh=hh, h=h, j=j, kb=kb, n=n, g=g: P.matmul(
                                pS[:, hh * 128:(hh + 1) * 128], lhsT=kT[g][h % 2][:, kb * 128:(kb + 1) * 128],
                                rhs=qT[:, j, n * 128:(n + 1) * 128], start=True, stop=True),
                                reads=[bkT, bqT], writes=[bpS])
                        pt, bpt = pT[pidx % 3], bpT[pidx % 3]
                        pidx += 1
                        fw.op("act", lambda pS=pS, pt=pt: A.activation(out=pt, in_=pS, func=AF.Exp, scale=0.125),
                              reads=[bpS], writes=[bpt])
                        mi = None
                        if kb == tq - 1:
                            mi = 2 if (halo_masks and kb < own0) else 0
                        elif kb == tq + 1:
                            mi = 3 if (halo_masks and kb >= own0 + ntl) else 1
                        if mi is not None and KATT >= 2:
                            mv = msk[:, mi * 128:(mi + 1) * 128].unsqueeze(1).to_broadcast([128, 4, 128])
                            fw.op("pool", lambda pt=pt, mv=mv: G.tensor_tensor(out=pt.rearrange("p (h q) -> p h q", h=4),
                                                                              in0=pt.rearrange("p (h q) -> p h q", h=4), in1=mv, op=ALU.mult),
                                  reads=[bpt, b_const], writes=[bpt])
                        if KATT < 3:
                            continue
                        fw.op("pe", lambda po=po, kb=kb, g=g, pt=pt, ii=ii, kbs=kbs: P.matmul(
                            po, lhsT=v2[:, kb, g * 128:(g + 1) * 128], rhs=pt, start=(ii == 0), stop=(ii == len(kbs) - 1)),
                            reads=[bv2, bpt], writes=[bpo])
                        fw.op("pe", lambda pd=pd, pt=pt, ii=ii, kbs=kbs: P.matmul(
                            pd, lhsT=ones_bf, rhs=pt, start=(ii == 0), stop=(ii == len(kbs) - 1)),
                            reads=[b_const, bpt], writes=[bpd])
                    if KATT < 4:
                        continue
                    es = esink[:, 4 * g:4 * g + 4].unsqueeze(2).to_broadcast([128, 4, 128])
                    fw.op("dve", lambda pd=pd, es=es: V.tensor_tensor(out=rec.rearrange("p (h q) -> p h q", h=4),
                                                                     in0=pd.rearrange("p (h q) -> p h q", h=4), in1=es, op=ALU.add),
                          reads=[bpd, b_const], writes=[brec])
                    fw.op("dve", lambda: V.reciprocal(out=rec, in_=rec), reads=[brec], writes=[brec])
                    for hh in range(4):
                        h = 4 * g + hh
                        pbase = (h % 2) * 64
                        j = h // 2
                        fw.op("dve", lambda po=po, hh=hh, pbase=pbase, j=j, n=n: V.tensor_tensor(
                            out=yat[pbase:pbase + 64, j, n * 128:(n + 1) * 128], in0=po[pbase:pbase + 64, hh * 128:(hh + 1) * 128],
                            in1=rec[pbase:pbase + 64, hh * 128:(hh + 1) * 128], op=ALU.mult),
                            reads=[bpo, brec], writes=[byat])

            if KSTOP <= 3:
                fw.barrier(); ar.reset(m0); return
            fw.barrier()
            ar.reset(m1)
            GT = ar.alloc(2 * 512).rearrange("p (c t) -> p c t", c=2)
            bGT = Buf()
            mg = ar.alloc(8 * 512).rearrange("p (c t) -> p c t", c=8)
            bmg = Buf()
            x1 = ar.alloc(4 * D, F32).rearrange("p (t d) -> p t d", t=4)
            bx1 = Buf()
            fT = ar.alloc(8 * 512).rearrange("p (k t) -> p k t", k=8)
            bfT = Buf()
            hf = ar.alloc(22 * 512).rearrange("p (c t) -> p c t", c=22)
            bhf = Buf()
            sg = [ar.alloc(512) for _ in range(2)]
            bsg = [Buf(), Buf()]
            tm = [ar.alloc(512, F32) for _ in range(2)]
            btm = [Buf(), Buf()]
            for b in range(ntl // 4):
                tok0 = (own0 + b * 4) * 128
                o0 = b * 512
                fw.dma("sp", lambda b=b: nc.sync.dma_start(out=x1, in_=xsrc_fn(own0 + b * 4, 4).rearrange("(t p) d -> p t d", p=128)),
                       writes=[bx1])
                for c in range(8):
                    wt, bw = wload_multi([wsrc(w_in, 8, 2304 + c * 128, 128), wsrc(w_in, 8, 3328 + c * 128, 128)])
                    wab, bwab = wload_multi([wsrc(w_hy, 4, c * 128, 128), wsrc(w_at, 4, c * 128, 128)])
                    for gi in range(2):
                        p, bp = ps()
                        for k in range(8):
                            fw.op("pe", lambda p=p, k=k, wt=wt, tok0=tok0, gi=gi: P.matmul(p, lhsT=wt[:, k, gi * 128:(gi + 1) * 128], rhs=hT[:, k, tok0:tok0 + 512],
                                                                                           start=(k == 0), stop=(k == 7)),
                                  reads=[bw, bhT], writes=[bp])
                        fw.op("act", lambda p=p, gi=gi: A.activation(out=GT[:, gi, :], in_=p, func=AF.Sigmoid), reads=[bp], writes=[bGT])
                    pa, bpa = ps()
                    pb, bpb = ps()
                    for k in range(4):
                        fw.op("pe", lambda pa=pa, k=k, wab=wab, o0=o0: P.matmul(pa, lhsT=wab[:, k, 0:128], rhs=hyT[:, k, o0:o0 + 512],
                                                                                start=(k == 0), stop=(k == 3)),
                              reads=[bwab, bhyT], writes=[bpa])
                    for k in range(4):
                        fw.op("pe", lambda pb=pb, k=k, wab=wab, o0=o0: P.matmul(pb, lhsT=wab[:, k, 128:256], rhs=yat[:, k, o0:o0 + 512],
                                                                                start=(k == 0), stop=(k == 3)),
                              reads=[bwab, byat], writes=[bpb])
                    ta, bta = tm[0], btm[0]
                    tb, btb = tm[1], btm[1]
                    fw.op("dve", lambda pa=pa, c=c, ta=ta: V.tensor_tensor(out=ta, in0=pa, in1=GT[:, 0, :], op=ALU.mult),
                          reads=[bpa, bGT], writes=[bta])
                    fw.op("dve", lambda pb=pb, c=c, tb=tb: V.tensor_tensor(out=tb, in0=pb, in1=GT[:, 1, :], op=ALU.mult),
                          reads=[bpb, bGT], writes=[btb])
                    fw.op("pool", lambda c=c, ta=ta, tb=tb: G.tensor_tensor(out=mg[:, c, :], in0=ta, in1=tb, op=ALU.add),
                          reads=[bta, btb], writes=[bmg])
                for nh_ in range(2):
                    wt, bw = wload(wsrc(w_o, 8, nh_ * 512, 512))
                    for tt in range(4):
                        p, bp = ps()
                        for k in range(8):
                            fw.op("pe", lambda p=p, k=k, wt=wt, tt=tt: P.matmul(p, lhsT=mg[:, k, tt * 128:(tt + 1) * 128], rhs=wt[:, k, :],
                                                                                start=(k == 0), stop=(k == 7)),
                                  reads=[bw, bmg], writes=[bp])
                        fw.op("dve", lambda p=p, tt=tt, nh_=nh_: V.tensor_tensor(out=x1[:, tt, nh_ * 512:(nh_ + 1) * 512], in0=p,
                                                                                 in1=x1[:, tt, nh_ * 512:(nh_ + 1) * 512], op=ALU.add),
                              reads=[bp, bx1], writes=[bx1])
                for tt in range(4):
                    rs, bsmall = rstd_of(x1[:, tt, :], D, 1, [bx1])
                    hb, bhb = hbR.next()
                    fw.op("dve", lambda tt=tt, rs=rs, hb=hb: V.scalar_tensor_tensor(out=hb, in0=x1[:, tt, :], scalar=rs, in1=g2,
                                                                                   op0=ALU.mult, op1=ALU.mult),
                          reads=[bx1, bsmall, b_const], writes=[bhb])
                    p, bp = ps()
                    pv = p.bitcast(BF16)
                    for k in range(8):
                        fw.op("pe", lambda pv=pv, k=k, hb=hb: P.transpose(out=pv[:, k * 128:(k + 1) * 128], in_=hb[:, k * 128:(k + 1) * 128], identity=ident),
                              reads=[bhb, b_const], writes=[bp])
                    fw.op("act", lambda pv=pv, tt=tt: A.copy(out=fT[:, :, tt * 128:(tt + 1) * 128], in_=pv.rearrange("p (k t) -> p k t", k=8)),
                          reads=[bp], writes=[bfT])
                for c in range(22):
                    wg, bwg = wload(wsrc(w_gate, 8, c * 128, 128))
                    wu, bwu = wload(wsrc(w_up, 8, c * 128, 128))
                    pg, bpg = ps()
                    pu, bpu = ps()
                    for k in range(8):
                        fw.op("pe", lambda pg=pg, k=k, wg=wg: P.matmul(pg, lhsT=wg[:, k, :], rhs=fT[:, k, :], start=(k == 0), stop=(k == 7)),
                              reads=[bwg, bfT], writes=[bpg])
                    for k in range(8):
                        fw.op("pe", lambda pu=pu, k=k, wu=wu: P.matmul(pu, lhsT=wu[:, k, :], rhs=fT[:, k, :], start=(k == 0), stop=(k == 7)),
                              reads=[bwu, bfT], writes=[bpu])
                    s_, bs_ = sg[c % 2], bsg[c % 2]
                    fw.op("act", lambda pg=pg, s_=s_: A.activation(out=s_, in_=pg, func=AF.Silu), reads=[bpg], writes=[bs_])
                    fw.op("dve", lambda pu=pu, s_=s_, c=c: V.tensor_tensor(out=hf[:, c, :], in0=pu, in1=s_, op=ALU.mult),
                          reads=[bpu, bs_], writes=[bhf])
                for nh_ in range(2):
                    for k0_, kn_ in ((0, 8), (8, 8), (16, 6)):
                        wt, bw = wload(wsrc(w_down, kn_, nh_ * 512, 512, k0=k0_))
                        for tt in range(4):
                            p, bp = ps()
                            for k in range(kn_):
                                fw.op("pe", lambda p=p, k=k, wt=wt, tt=tt, k0_=k0_, kn_=kn_: P.matmul(
                                    p, lhsT=hf[:, k0_ + k, tt * 128:(tt + 1) * 128], rhs=wt[:, k, :], start=(k == 0), stop=(k == kn_ - 1)),
                                    reads=[bw, bhf], writes=[bp])
                            fw.op("dve", lambda p=p, tt=tt, nh_=nh_: V.tensor_tensor(out=x1[:, tt, nh_ * 512:(nh_ + 1) * 512], in0=p,
                                                                                     in1=x1[:, tt, nh_ * 512:(nh_ + 1) * 512], op=ALU.add),
                                  reads=[bp, bx1], writes=[bx1])
                fw.dma("act", lambda b=b: nc.scalar.dma_start(out=ydst[b * 512:(b + 1) * 512, :].rearrange("(t p) d -> p t d", p=128), in_=x1),
                       reads=[bx1])
            fw.barrier()
            ar.reset(m0)


        def hyena_phase(tag, xsrc_fn, L, cg, own_static, hyT, bhyT):
            C = HC[tag]
            R = L // 128
            GP = 16
            CP = GP * cg
            NI = 2 * cg * (R + 1)
            NPASS = 512 // CP
            NT1 = 512 // CP
            own_t10 = 0
            NIo = 2 * cg * 17
            uT_d = nc.dram_tensor("uT_" + tag, [1536, L + 2], BF16).ap()
            u2_d = nc.dram_tensor("u2_" + tag, [512, 2304], BF16).ap()
            h2_d = nc.dram_tensor("h2_" + tag, [2, 64, L], BF16).ap()
            buT, bh2d = Buf(), Buf()
            m0 = ar.mark()
            whyT = ar.alloc(12 * 8 * 128).rearrange("p (c k n) -> p c k n", c=12, k=8)
            bwhy = Buf()
            for c in range(12):
                fw.dma("pool", lambda c=c: G.dma_start(out=whyT[:, c], in_=w_in_hy.rearrange("(k p) n -> p k n", p=128)[:, :, c * 128:(c + 1) * 128]),
                       writes=[bwhy])
            hTb = [ar.alloc(8 * 512).rearrange("p (k t) -> p k t", k=8) for _ in range(2)]
            bhTb = [Buf(), Buf()]
            stgR = Rot(2, lambda: ar.alloc(12 * 512).rearrange("p (c t) -> p c t", c=12))
            stg, bstg = stgR.items[0]
            zt_ = ar.alloc(16)
            bzt = Buf()
            fw.op("dve", lambda: V.memset(zt_, 0.0), writes=[bzt])
            for col in (0, L + 1):
                fw.dma("sp", lambda col=col: nc.sync.dma_start(out=uT_d[:, col:col + 1].rearrange("(c p) o -> p (c o)", p=128), in_=zt_[:, 0:12], allow_slow_non_contiguous=True),
                       reads=[bzt], writes=[buT])
            for b in range(L // 512):
                hT_b, bh = hTb[b % 2], bhTb[b % 2]
                norm_transpose(xsrc_fn, [(b * 4 + i, i) for i in range(4)], hT_b, bh, g1)
                stg, bstg = stgR.next()
                for c in range(12):
                    p, bp = ps()
                    for k in range(8):
                        fw.op("pe", lambda p=p, c=c, k=k, hT_b=hT_b: P.matmul(p, lhsT=whyT[:, c, k, :], rhs=hT_b[:, k, :], start=(k == 0), stop=(k == 7)),
                              reads=[bwhy, bh], writes=[bp])
                    if c % 2 == 0:
                        fw.op("act", lambda p=p, c=c, stg=stg: A.copy(out=stg[:, c, :], in_=p), reads=[bp], writes=[bstg])
                    else:
                        fw.op("dve", lambda p=p, c=c, stg=stg: V.tensor_copy(out=stg[:, c, :], in_=p), reads=[bp], writes=[bstg])
                fw.dma("act", lambda b=b, stg=stg: nc.scalar.dma_start(out=uT_d[:, 1 + b * 512:1 + (b + 1) * 512].rearrange("(c p) t -> p c t", p=128), in_=stg),
                       reads=[bstg], writes=[buT])
            if not own_static:
                for b in range(5):
                    nt_ = 4 if b < 4 else 2
                    hT_b, bh = hTb[b % 2], bhTb[b % 2]
                    norm_transpose(lambda t: xs[t * 128:(t + 1) * 128, :], [(b * 4 + i, i) for i in range(nt_)], hT_b, bh, g1)
                    for c in range(4):
                        p, bp = ps()
                        for k in range(8):
                            fw.op("pe", lambda p=p, c=c, k=k, hT_b=hT_b, nt_=nt_: P.matmul(p[:, 0:nt_ * 128], lhsT=whyT[:, 4 + c, k, :], rhs=hT_b[:, k, 0:nt_ * 128],
                                                                                       start=(k == 0), stop=(k == 7)), reads=[bwhy, bh], writes=[bp])
                        fw.op("act", lambda p=p, c=c, stg=stg: A.copy(out=stg[:, c, :], in_=p), reads=[bp], writes=[bstg])
                    fw.dma("act", lambda b=b, nt_=nt_, stg=stg: nc.scalar.dma_start(out=u2_d[:, b * 512:b * 512 + nt_ * 128].rearrange("(c p) t -> p c t", p=128),
                                                                       in_=stg[:, 0:4, 0:nt_ * 128]), reads=[bstg], writes=[buT])
            fw.barrier()
            ar.reset(m0)

            w1s = ar.alloc(128, F32)
            w2s = ar.alloc(128, F32)
            fbt = ar.alloc(4, F32)
            frb = ar.alloc(2, F32)
            bmw = Buf()
            fw.op("dve", lambda: V.memset(w1s, 0.0), writes=[bmw])
            fw.op("dve", lambda: V.memset(w2s, 0.0), writes=[bmw])
            fw.dma("sp", lambda: nc.sync.dma_start(out=w1s[0:33, 0:64], in_=fw1), writes=[bmw])
            fw.dma("sp", lambda: nc.sync.dma_start(out=w1s[33:66, 64:128], in_=fw1), writes=[bmw])
            fw.dma("sp", lambda: nc.sync.dma_start(out=w2s[0:64, 0:64], in_=fw2), writes=[bmw])
            fw.dma("sp", lambda: nc.sync.dma_start(out=w2s[64:128, 64:128], in_=fw2), writes=[bmw])
            fw.dma("sp", lambda: nc.sync.dma_start(out=fbt[0:64, 0:3], in_=fb), writes=[bmw])
            fw.dma("sp", lambda: nc.sync.dma_start(out=fbt[64:128, 0:3], in_=fb), writes=[bmw])
            fw.op("dve", lambda: V.tensor_scalar(frb[:, 0:2], fbt[:, 0:2], fbt[:, 2:3], None, op0=ALU.mult), reads=[bmw], writes=[bmw])
            ztR = Rot(2, lambda: ar.alloc(512, F32))
            a1R = Rot(2, lambda: ar.alloc(512, F32))
            mkR = Rot(2, lambda: ar.alloc(512, F32))
            h1R = Rot(2, lambda: ar.alloc(512, F32))
            h2R = Rot(2, lambda: ar.alloc(512))

            def sin_layer(p, bp, col, out_ap, bout):
                a1, ba1 = a1R.next()
                mk, bmk = mkR.next()
                fw.op("dve", lambda: V.tensor_scalar(a1, p, fbt[:, 2:3], frb[:, col:col + 1], op0=ALU.mult, op1=ALU.add),
                      reads=[bp, bmw], writes=[ba1])
                for _ in range(2):
                    fw.op("dve", lambda: V.tensor_scalar(mk, a1, math.pi, -2 * math.pi, op0=ALU.is_gt, op1=ALU.mult), reads=[ba1], writes=[bmk])
                    fw.op("dve", lambda: V.tensor_tensor(a1, a1, mk, op=ALU.add), reads=[ba1, bmk], writes=[ba1])
                    fw.op("dve", lambda: V.tensor_scalar(mk, a1, -math.pi, 2 * math.pi, op0=ALU.is_lt, op1=ALU.mult), reads=[ba1], writes=[bmk])
                    fw.op("dve", lambda: V.tensor_tensor(a1, a1, mk, op=ALU.add), reads=[ba1, bmk], writes=[ba1])
                fw.op("act", lambda: A.activation(out=out_ap, in_=a1, func=AF.Sin), reads=[ba1], writes=[bout])

            def trunk_block(blk):
                z_, bz_ = ztR.next()
                hh_, bhh_ = h2R.next()
                h1, bh1 = h1R.next()
                for pn in range(2):
                    fw.dma("sp", lambda pn=pn: nc.sync.dma_start(out=z_[pn * 33:(pn + 1) * 33, :], in_=C["zf"][pn, :, blk * 512:(blk + 1) * 512]), writes=[bz_])
                p, bp = ps()
                fw.op("pe", lambda: P.matmul(p, lhsT=w1s[0:66, :], rhs=z_[0:66, :], start=True, stop=True), reads=[bmw, bz_], writes=[bp])
                sin_layer(p, bp, 0, h1, bh1)
                p2, bp2 = ps()
                fw.op("pe", lambda: P.matmul(p2, lhsT=w2s, rhs=h1, start=True, stop=True), reads=[bmw, bh1], writes=[bp2])
                sin_layer(p2, bp2, 1, hh_, bhh_)
                for pn in range(2):
                    fw.dma("act", lambda pn=pn: nc.scalar.dma_start(out=h2_d[pn, :, blk * 512:(blk + 1) * 512], in_=hh_[pn * 64:(pn + 1) * 64, :]),
                           reads=[bhh_], writes=[bh2d])

            for blk in range(L // 512):
                trunk_block(blk)
            fw.barrier()
            ar.reset(m0)

            if not own_static:
                zd = nc.dram_tensor("zd_" + tag, [2, 128, 512, R], BF16).ap()
                bzd = Buf()
                stT = ar.alloc(128 * R).rearrange("p (c t) -> p c t", c=128)
                bstT = Buf()
                UppR = Rot(2, lambda: ar.alloc(2050))
                ucfR = Rot(2, lambda: ar.alloc(2048, F32))
                ucbR = Rot(2, lambda: ar.alloc(2048))
                cwt_p = ar.alloc(4, F32)
                bcwt_p = Buf()

                def pre_piece(row0, pc):
                    Upp, bUpp = UppR.next()
                    ucf_p, bucf_p = ucfR.next()
                    ucb_p, bucb_p = ucbR.next()
                    s0 = pc * 2048
                    fw.dma("sp", lambda: nc.sync.dma_start(out=Upp, in_=uT_d[row0:row0 + 128, s0:s0 + 2050]), reads=[buT], writes=[bUpp])
                    fw.op("act", lambda: A.activation(out=ucf_p, in_=Upp[:, 1:2049], func=AF.Identity, bias=cwt_p[:, 3:4], scale=cwt_p[:, 1:2]),
                          reads=[bUpp, bcwt_p], writes=[bucf_p])
                    fw.op("dve", lambda: V.scalar_tensor_tensor(out=ucf_p, in0=Upp[:, 0:2048], scalar=cwt_p[:, 0:1], in1=ucf_p, op0=ALU.mult, op1=ALU.add),
                          reads=[bUpp, bcwt_p, bucf_p], writes=[bucf_p])
                    fw.op("dve", lambda: V.scalar_tensor_tensor(out=ucb_p, in0=Upp[:, 2:2050], scalar=cwt_p[:, 2:3], in1=ucf_p, op0=ALU.mult, op1=ALU.add),
                          reads=[bUpp, bcwt_p, bucf_p], writes=[bucb_p])
                    for j0 in (0, 8):
                        p, bp = ps()
                        pv = p.bitcast(BF16)
                        for j in range(8):
                            fw.op("pe", lambda pv=pv, j=j, j0=j0: P.transpose(out=pv[:, j * 128:(j + 1) * 128], in_=ucb_p[:, (j0 + j) * 128:(j0 + j + 1) * 128], identity=ident),
                                  reads=[bucb_p, b_const], writes=[bp])
                        t1a = pc * 16 + j0
                        fw.op("act", lambda pv=pv, t1a=t1a: A.copy(out=stT[:, :, t1a:t1a + 8], in_=pv.rearrange("p (t c) -> p c t", c=128)), reads=[bp], writes=[bstT])

                for si, rbase in enumerate((0, 1024)):
                    for chk in range(4):
                        row0 = rbase + chk * 128
                        fw.dma("sp", lambda row0=row0: nc.sync.dma_start(out=cwt_p, in_=cw_all[row0:row0 + 128, :]), writes=[bcwt_p])
                        for pc in range(L // 2048):
                            pre_piece(row0, pc)
                        fw.dma("act", lambda si=si, chk=chk: nc.scalar.dma_start(out=zd[si, :, chk * 128:(chk + 1) * 128, :], in_=stT), reads=[bstT], writes=[bzd])
                fw.barrier()
                ar.reset(m0)

            bhc = Buf()
            Gf = ar.alloc(256)
            S2m = ar.alloc(12 * 128).rearrange("p (a b) -> p a b", a=12)
            INV = ar.alloc(4 * NI).rearrange("p (a b) -> p a b", a=4)
            S1I = ar.alloc(4 * 128).rearrange("p (a b) -> p a b", a=4)
            INVo = INV
            if not own_static:
                INVo = ar.alloc(4 * NIo).rearrange("p (a b) -> p a b", a=4)
                fw.dma("pool", lambda: G.dma_start(out=INVo, in_=C["INVo"]), writes=[bhc])
            if not own_static:
                Btab = ar.alloc(2 * 512, F32).rearrange("p (a c) -> p a c", a=2)
                fw.dma("sp", lambda: nc.sync.dma_start(out=Btab, in_=C["Btab"]), writes=[bhc])
            bn_t = ar.alloc(CP, F32)
            bbn = Buf()
            if own_static:
                negt = ar.alloc(2 * R, F32)
                absd = ar.alloc(512, F32)
            hbias = ar.alloc(1024, F32)
            ones_f = ar.alloc(128, F32)
            fw.dma("pool", lambda: G.dma_start(out=Gf, in_=C["Gf"]), writes=[bhc])
            fw.dma("pool", lambda: G.dma_start(out=S2m, in_=C["S2"]), writes=[bhc])
            fw.dma("pool", lambda: G.dma_start(out=INV, in_=C["INV"]), writes=[bhc])
            fw.dma("pool", lambda: G.dma_start(out=S1I, in_=C["S1I"]), writes=[bhc])
            if own_static:
                fw.dma("sp", lambda: nc.sync.dma_start(out=negt, in_=C["negt"].rearrange("p a b -> p (a b)")), writes=[bhc])
                fw.dma("sp", lambda: nc.sync.dma_start(out=absd, in_=absd_bc), writes=[bhc])
            fw.dma("sp", lambda: nc.sync.dma_start(out=hbias, in_=hb_bc), writes=[bhc])
            fw.op("dve", lambda: V.memset(ones_f, 1.0), writes=[bhc])
            fw.barrier()
            zb = ar.alloc(R * CP).rearrange("p (c t) -> p c t", c=CP)
            x1b = ar.alloc(R * CP).rearrange("p (c t) -> p c t", c=CP)
            x2b = ar.alloc(16 * CP).rearrange("p (c t) -> p c t", c=CP)
            bzb, bx1b, bx2b = Buf(), Buf(), Buf()
            kf = [[ar.alloc(R * CP).rearrange("p (c t) -> p c t", c=CP) for _ in range(2)] for _ in range(2)]
            bkf = Buf()
            win = [ar.alloc(R * CP).rearrange("p (c t) -> p c t", c=CP) for _ in range(2)]
            bwin = Buf()
            w3s = ar.alloc(4 * CP).rearrange("p (d o c) -> p d o c", d=2, o=2)
            bw3 = Buf()
            if own_static:
                h2p = [ar.alloc(2048)] * 2
                bh2p = [Buf()] * 2
            else:
                h2p = [ar.alloc(2048) for _ in range(2)]
                bh2p = [Buf(), Buf()]
            acc = ar.alloc(2 * CP, F32)
            rn = ar.alloc(2 * CP, F32)
            bacc, brn = Buf(), Buf()
            Kt = ar.alloc(4 * GP * 128).rearrange("p (h r g k) -> p h r g k", h=2, r=2, g=GP)
            bK = Buf()
            Bsb = [ar.alloc(GP * 256).rearrange("p (g k) -> p g k", g=GP) for _ in range(2)]
            bBsb = [Buf(), Buf()]
            ZsR = Rot(2, lambda: ar.alloc(4 * 512).rearrange("p (h r n) -> p h r n", h=2, r=2))
            YsR = Rot(2, lambda: ar.alloc(4 * 512).rearrange("p (h r n) -> p h r n", h=2, r=2))
            tYR = Rot(4, lambda: ar.alloc(512))
            Wsb = ar.alloc(GP * NI).rearrange("p (g n) -> p g n", g=GP)
            bWsb = Buf()
            U = [ar.alloc(1026)] * 2
            bU = [Buf()] * 2
            ucf = ar.alloc(1024, F32)
            ucb = ar.alloc(1024)
            bucf, bucb = Buf(), Buf()
            cwt = ar.alloc(4, F32)
            bcwt = Buf()
            tAR = Rot(2, lambda: ar.alloc(512, F32))
            tBR = Rot(2, lambda: ar.alloc(512, F32))
            yst = ar.alloc(16 * 128).rearrange("p (t c) -> p t c", c=128)
            byst = Buf()
            ucount = [0]

            def load_stream(row0, tok0, ntok, dst, bdst, src=None, crow=None):
                src = uT_d if src is None else src
                crow = row0 if crow is None else crow
                fw.dma("sp", lambda: nc.sync.dma_start(out=cwt[0:CP, :], in_=cw_all[crow:crow + CP, :]), writes=[bcwt])
                for pc in range(ntok // 1024):
                    u_, bu_ = U[ucount[0] % 2], bU[ucount[0] % 2]
                    ucount[0] += 1
                    s0 = tok0 + pc * 1024
                    fw.dma("sp", lambda u_=u_, s0=s0: nc.sync.dma_start(out=u_[0:CP, :], in_=src[row0:row0 + CP, s0:s0 + 1026]), reads=[buT], writes=[bu_])
                    fw.op("act", lambda u_=u_: A.activation(out=ucf[0:CP, :], in_=u_[0:CP, 1:1025], func=AF.Identity, bias=cwt[0:CP, 3:4], scale=cwt[0:CP, 1:2]),
                          reads=[bu_, bcwt], writes=[bucf])
                    fw.op("dve", lambda u_=u_: V.scalar_tensor_tensor(out=ucf[0:CP, :], in0=u_[0:CP, 0:1024], scalar=cwt[0:CP, 0:1], in1=ucf[0:CP, :],
                                                                     op0=ALU.mult, op1=ALU.add), reads=[bu_, bcwt, bucf], writes=[bucf])
                    fw.op("dve", lambda u_=u_: V.scalar_tensor_tensor(out=ucb[0:CP, :], in0=u_[0:CP, 2:1026], scalar=cwt[0:CP, 2:3], in1=ucf[0:CP, :],
                                                                     op0=ALU.mult, op1=ALU.add), reads=[bu_, bcwt, bucf], writes=[bucb])
                    per = min(8, 1024 // CP)
                    for j0 in range(0, 8, per):
                        p, bp = ps()
                        pv = p.bitcast(BF16)
                        for j in range(per):
                            fw.op("pe", lambda pv=pv, j=j, j0=j0: P.transpose(out=pv[:, j * CP:(j + 1) * CP], in_=ucb[0:CP, (j0 + j) * 128:(j0 + j + 1) * 128],
                                                                             identity=ident[0:CP, 0:CP]), reads=[bucb, b_const], writes=[bp])
                        t1a = pc * 8 + j0
                        fw.op("act", lambda pv=pv, t1a=t1a, per=per: A.copy(out=dst[:, :, t1a:t1a + per], in_=pv[:, 0:per * CP].rearrange("p (t c) -> p c t", c=CP)),
                              reads=[bp], writes=[bdst])

            def s1_stage(src, bsrc, Bd, bBd):
                for g in range(0, GP, 2):
                    p, bp = ps()
                    for gg in range(2):
                        lt = src[:, (g + gg) * cg:(g + gg + 1) * cg, :].rearrange("p c t -> p (c t)")
                        fw.op("pe", lambda p=p, gg=gg, lt=lt: P.matmul(p[:, gg * 256:(gg + 1) * 256], lhsT=lt, rhs=Gf, start=True, stop=True),
                              reads=[bsrc, bhc], writes=[bp])
                    fw.op("act", lambda p=p, g=g: A.copy(out=Bd[:, g:g + 2, :], in_=p.rearrange("p (g k) -> p g k", g=2)), reads=[bp], writes=[bBd])

            def s2_stage(g0, srcs):
                outs = {}
                for h in range(2):
                    for ri in range(2):
                        p, bp = ps()
                        nmm = 2 * len(srcs)
                        i = 0
                        for (Bd, bBd, neg) in srcs:
                            base = h * 6 + (3 if neg else 0)
                            BR = Bd[:, g0:g0 + 4, 0:128]
                            BI = Bd[:, g0:g0 + 4, 128:256]
                            pairs = ((base + 0, BR), (base + 2, BI)) if ri == 0 else ((base + 0, BI), (base + 1, BR))
                            for (mi, mv) in pairs:
                                fw.op("pe", lambda p=p, mi=mi, mv=mv, i=i, nmm=nmm: P.matmul(p, lhsT=S2m[:, mi, :], rhs=mv, start=(i == 0), stop=(i == nmm - 1)),
                                      reads=[bBd, bhc], writes=[bp])
                                i += 1
                        outs[(h, ri)] = (p, bp)
                return outs

            for pi in range(NPASS):
                c0 = pi * CP
                if own_static:
                    load_stream(1024 + c0, 0, L, zb, bzb)
                    load_stream(c0, 0, L, x1b, bx1b)
                else:
                    fw.dma("sp", lambda c0=c0: nc.sync.dma_start(out=zb, in_=zd[1, :, c0:c0 + CP, :]), reads=[bzd], writes=[bzb])
                    fw.dma("sp", lambda c0=c0: nc.sync.dma_start(out=x1b, in_=zd[0, :, c0:c0 + CP, :]), reads=[bzd], writes=[bx1b])
                for d_ in range(2):
                    for o_ in range(2):
                        cc_ = o_ * 1024 + d_ * 512 + c0
                        fw.dma("pool", lambda d_=d_, o_=o_, cc_=cc_: G.dma_start(out=w3s[0:64, d_, o_, :], in_=fw3[:, cc_:cc_ + CP]), writes=[bw3])
                for pn in range(2):
                    if own_static:
                        for l1 in range(R):
                            fw.op("act", lambda pn=pn, l1=l1, c0=c0: A.activation(out=win[pn][:, :, l1], in_=absd[:, c0:c0 + CP], func=AF.Exp,
                                                                                 scale=negt[:, pn * R + l1:pn * R + l1 + 1]), reads=[bhc], writes=[bwin])
                    else:
                        fw.dma("pool", lambda pn=pn, c0=c0: G.dma_start(out=win[pn].rearrange("p c t -> p (c t)").unsqueeze(1),
                                                                       in_=C["Atab"][pn:pn + 1, c0 * R:(c0 + CP) * R].partition_broadcast(128)),
                               writes=[bwin])
                        fw.op("pool", lambda pn=pn, c0=c0: G.tensor_tensor(out=win[pn], in0=win[pn],
                                                                          in1=Btab[:, pn, c0:c0 + CP].unsqueeze(2).to_broadcast([128, CP, R]), op=ALU.mult),
                              reads=[bwin, bhc], writes=[bwin])
                hcount = 0
                nl = max(1, 512 // (2 * CP))
                for pn in range(2):
                    for pc in range(L // 2048):
                        hp, bhp = h2p[hcount % 2], bh2p[hcount % 2]
                        hcount += 1
                        fw.dma("sp", lambda hp=hp, pn=pn, pc=pc: nc.sync.dma_start(out=hp[0:64, :], in_=h2_d[pn, :, pc * 2048:(pc + 1) * 2048]), reads=[bh2d], writes=[bhp])
                        for la in range(0, 16, nl):
                            p, bp = ps()
                            for j in range(nl):
                                fw.op("pe", lambda p=p, j=j, la=la, hp=hp, pn=pn: P.matmul(p[:, j * 2 * CP:(j + 1) * 2 * CP], lhsT=hp[0:64, (la + j) * 128:(la + j + 1) * 128],
                                                                                       rhs=w3s[0:64, pn].rearrange("p o c -> p (o c)"), start=True, stop=True),
                                      reads=[bhp, bw3], writes=[bp])
                            l1a = pc * 16 + la
                            for o in range(2):
                                fw.op("dve", lambda p=p, o=o, pn=pn, l1a=l1a: V.scalar_tensor_tensor(
                                    out=kf[o][pn][:, :, l1a:l1a + nl], in0=win[pn][:, :, l1a:l1a + nl], scalar=0.05,
                                    in1=p[:, 0:nl * 2 * CP].rearrange("p (l o c) -> p o c l", o=2, c=CP)[:, o], op0=ALU.add, op1=ALU.mult),
                                    reads=[bp, bwin], writes=[bkf])
                for o in range(2):
                    fw.op("dve", lambda o=o: V.memset(kf[o][1][0:1, :, 0:1], 0.0), writes=[bkf])
                for o in range(2):
                    for pn in range(2):
                        s1_stage(kf[o][pn], bkf, Bsb[pn], bBsb[pn])
                    for g0 in range(0, GP, 4):
                        outs = s2_stage(g0, [(Bsb[0], bBsb[0], False), (Bsb[1], bBsb[1], True)])
                        for (h, ri), (p, bp) in outs.items():
                            fw.op("act", lambda p=p, h=h, ri=ri, g0=g0: A.copy(out=Kt[:, h, ri, g0:g0 + 4, :], in_=p.rearrange("p (g k) -> p g k", g=4)),
                                  reads=[bp], writes=[bK])
                    if o == 0:
                        for o2 in range(2):
                            for pn in range(2):
                                dsta = (acc if pn == 0 else rn)[:, o2 * CP:(o2 + 1) * CP]
                                fw.op("dve", lambda o2=o2, pn=pn, dsta=dsta: V.tensor_reduce(out=dsta, in_=kf[o2][pn], axis=AX.X, op=ALU.add,
                                                                                         apply_absolute_value=True), reads=[bkf], writes=[bacc])
                        fw.op("dve", lambda: V.tensor_tensor(out=acc, in0=acc, in1=rn, op=ALU.add), reads=[bacc], writes=[bacc])
                        p, bp = ps()
                        fw.op("pe", lambda p=p: P.matmul(p[:, 0:2 * CP], lhsT=ones_f, rhs=acc, start=True, stop=True), reads=[bacc, bhc], writes=[bp])
                        fw.op("dve", lambda p=p: V.reciprocal(out=rn, in_=p[:, 0:2 * CP]), reads=[bp], writes=[brn])
                        if not own_static:
                            fw.op("dve", lambda p=p, c0=c0: V.tensor_tensor(out=bn_t, in0=p[:, CP:2 * CP], in1=hbias[:, 512 + c0:512 + c0 + CP], op=ALU.mult),
                                  reads=[bp, bhc], writes=[bbn])

                        if own_static:
                            load_stream(512 + c0, 0, 2048, x2b, bx2b)
                        else:
                            load_stream(c0, 127, 2048, x2b, bx2b, src=u2_d, crow=512 + c0)
                    if (not own_static) and o == 1:
                        for h in range(2):
                            fw.op("dve", lambda h=h: V.tensor_tensor(out=Kt[:, h, 0], in0=Kt[:, h, 0], in1=bn_t.unsqueeze(2).to_broadcast([128, GP, 128]), op=ALU.add),
                                  reads=[bK, bbn], writes=[bK])
                    s1_stage(zb, bzb, Bsb[0], bBsb[0])
                    IVm, NIc = (INV, NI) if (o == 0 or own_static) else (INVo, NIo)

                    def conv_a(g0):
                        outs = s2_stage(g0, [(Bsb[0], bBsb[0], False)])
                        Zs, bZs = ZsR.next()
                        Ys, bYs = YsR.next()
                        for (h, ri), (p, bp) in outs.items():
                            fw.op("act", lambda p=p, h=h, ri=ri, Zs=Zs: A.copy(out=Zs[:, h, ri, :], in_=p), reads=[bp], writes=[bZs])
                        for h in range(2):
                            KR = Kt[:, h, 0, g0:g0 + 4, :].rearrange("p g k -> p (g k)")
                            KI = Kt[:, h, 1, g0:g0 + 4, :].rearrange("p g k -> p (g k)")
                            ZR = Zs[:, h, 0, :]
                            ZI = Zs[:, h, 1, :]
                            ta_, bta_ = tYR.next()
                            tb_, btb_ = tYR.next()
                            fw.op("dve", lambda ZR=ZR, KR=KR, ta_=ta_: V.tensor_tensor(out=ta_, in0=ZR, in1=KR, op=ALU.mult), reads=[bZs, bK], writes=[bta_])
                            fw.op("pool", lambda ZI=ZI, KI=KI, tb_=tb_: G.tensor_tensor(out=tb_, in0=ZI, in1=KI, op=ALU.mult), reads=[bZs, bK], writes=[btb_])
                            fw.op("dve", lambda h=h, Ys=Ys, ta_=ta_, tb_=tb_: V.tensor_tensor(out=Ys[:, h, 0, :], in0=ta_, in1=tb_, op=ALU.subtract), reads=[bta_, btb_], writes=[bYs])
                            tc_, btc_ = tYR.next()
                            td_, btd_ = tYR.next()
                            fw.op("dve", lambda ZR=ZR, KI=KI, tc_=tc_: V.tensor_tensor(out=tc_, in0=ZR, in1=KI, op=ALU.mult), reads=[bZs, bK], writes=[btc_])
                            fw.op("pool", lambda ZI=ZI, KR=KR, td_=td_: G.tensor_tensor(out=td_, in0=ZI, in1=KR, op=ALU.mult), reads=[bZs, bK], writes=[btd_])
                            fw.op("dve", lambda h=h, Ys=Ys, tc_=tc_, td_=td_: V.tensor_tensor(out=Ys[:, h, 1, :], in0=tc_, in1=td_, op=ALU.add), reads=[btc_, btd_], writes=[bYs])
                        return Ys, bYs

                    def conv_b(g0, Ys, bYs):
                        for gl in range(4):
                            p, bp = ps()
                            i = 0
                            for h in range(2):
                                for ri in range(2):
                                    fw.op("pe", lambda p=p, h=h, ri=ri, gl=gl, i=i, NIc=NIc, IVm=IVm: P.matmul(p[:, 0:NIc], lhsT=Ys[:, h, ri, gl * 128:(gl + 1) * 128], rhs=IVm[:, 2 * h + ri, :],
                                                                                           start=(i == 0), stop=(i == 3)), reads=[bYs, bhc], writes=[bp])
                                    i += 1
                            fw.op("act", lambda p=p, gl=gl, NIc=NIc: A.copy(out=Wsb[:, g0 + gl, 0:NIc], in_=p[:, 0:NIc]), reads=[bp], writes=[bWsb])

                    prev = None
                    for g0 in list(range(0, GP, 4)) + [None]:
                        cur = None
                        if g0 is not None:
                            cur = (g0,) + conv_a(g0)
                        if prev is not None:
                            conv_b(*prev)
                        prev = cur
                    folded = (not own_static) and o == 1
                    if folded:
                        Wv = Wsb[:, :, 0:NIo].rearrange("p g (r c t) -> p g r c t", r=2, c=cg)
                    else:
                        Wv = Wsb.rearrange("p g (r c t) -> p g r c t", r=2, c=cg)
                    NT1b = min(NT1, 16)
                    blocks = range(0, R, NT1) if o == 0 else range(0, 16, NT1b)
                    for t1a in blocks:
                        nt = NT1 if o == 0 else NT1b
                        ncol = nt * CP
                        p, bp = ps()
                        for i, (mi, ri, sh) in enumerate(((0, 0, 1), (1, 1, 1), (2, 0, 0), (3, 1, 0))):
                            mv = Wv[:, :, ri, :, t1a + sh:t1a + sh + nt]
                            fw.op("pe", lambda p=p, mi=mi, mv=mv, i=i, ncol=ncol: P.matmul(p[:, 0:ncol], lhsT=S1I[:, mi, :], rhs=mv, start=(i == 0), stop=(i == 3)),
                                  reads=[bWsb, bhc], writes=[bp])
                        cv = p[:, 0:ncol].rearrange("p (c t) -> p c t", t=nt)
                        tA, btA = tAR.next()
                        tB, btB = tBR.next()
                        tAf = tA[:, 0:ncol]
                        tBf = tB[:, 0:ncol]
                        tAv = tAf.rearrange("p (c t) -> p c t", c=CP)
                        tBv = tBf.rearrange("p (c t) -> p c t", c=CP)
                        rnb = rn[:, o * CP:(o + 1) * CP].unsqueeze(2).to_broadcast([128, CP, nt])
                        hbb = hbias[:, o * 512 + c0:o * 512 + c0 + CP].unsqueeze(2).to_broadcast([128, CP, nt])
                        fw.op("dve", lambda cv=cv, tAv=tAv, rnb=rnb: V.tensor_tensor(out=tAv, in0=cv, in1=rnb, op=ALU.mult), reads=[bp, brn], writes=[btA])
                        if not folded:
                            zin = zb[:, :, t1a:t1a + nt]
                            fw.op("pool", lambda zin=zin, tBv=tBv, hbb=hbb: G.tensor_tensor(out=tBv, in0=zin, in1=hbb, op=ALU.mult), reads=[bzb, bhc], writes=[btB])
                            fw.op("dve", lambda tAf=tAf, tBf=tBf: V.tensor_tensor(out=tAf, in0=tAf, in1=tBf, op=ALU.add), reads=[btA, btB], writes=[btA])
                        if o == 0:
                            gate = x1b[:, :, t1a:t1a + nt]
                            fw.op("dve", lambda zin=zin, tAv=tAv, gate=gate: V.tensor_tensor(out=zin, in0=tAv, in1=gate, op=ALU.mult),
                                  reads=[btA, bx1b], writes=[bzb])
                        else:
                            tl = t1a
                            gate = x2b[:, :, tl:tl + nt]
                            co = c0 % 128
                            fw.op("dve", lambda tl=tl, tAv=tAv, gate=gate, co=co, nt=nt: V.tensor_tensor(out=yst[:, tl:tl + nt, co:co + CP].rearrange("p t c -> p c t"), in0=tAv, in1=gate, op=ALU.mult),
                                  reads=[btA, bx2b], writes=[byst])
                if (c0 + CP) % 128 == 0:
                    ch = c0 // 128
                    for j0 in (0, 8):
                        p, bp = ps()
                        pv = p.bitcast(BF16)
                        for j in range(8):
                            fw.op("pe", lambda pv=pv, j=j, j0=j0: P.transpose(out=pv[:, j * 128:(j + 1) * 128], in_=yst[:, j0 + j, :], identity=ident),
                                  reads=[byst, b_const], writes=[bp])
                        fw.op("act", lambda pv=pv, ch=ch, j0=j0: A.copy(out=hyT[:, ch, j0 * 128:(j0 + 8) * 128], in_=pv), reads=[bp], writes=[bhyT])
            fw.barrier()
            ar.reset(m0)

        mtop = ar.mark()
        hyT_s = ar.alloc(4 * NTOK).rearrange("p (k t) -> p k t", k=4)
        bhy_s = Buf()
        fw.op("dve", lambda: V.memset(hyT_s, 0.0), writes=[bhy_s])
        mp = ar.mark()
        hyT_p = ar.alloc(4 * NTOK).rearrange("p (k t) -> p k t", k=4)
        bhy_p = Buf()
        fw.op("dve", lambda: V.memset(hyT_p, 0.0), writes=[bhy_p])
        if KHY & 1:
            hyena_phase("p", lambda t, n=1: xp[t * 128:(t + n) * 128, :], 2048, 8, True, hyT_p, bhy_p)
        if KHY & 2:
            hyena_phase("s", lambda t, n=1: xs_all[t * 128:(t + n) * 128, :], 16384, 1, False, hyT_s, bhy_s)
        for _ in range(NWB):
            wbuf.append(ar.alloc(8 * 512))
        run_group(lambda t, n=1: xp[t * 128:(t + n) * 128, :], 16, 0, 16, ropep, yp, False, hyT_p, bhy_p)
        fw.barrier()
        ar.reset(mp)
        del wbuf[:]
        for _ in range(NWB):
            wbuf.append(ar.alloc(8 * 512))
        if KGROUPS >= 2:
            run_group(lambda t, n=1: xs[t * 128:(t + n) * 128, :], 18, 1, 16, ropes, ys, True, hyT_s, bhy_s)
        fw.barrier()
        fw.replay(block)
    return nc


def _host_consts(core):
    inv = 10000.0 ** (-np.arange(0, 64, 2, dtype=np.float32) / 64)

    def rope_tab(pos):
        ang = pos.astype(np.float32)[:, None] * inv[None, :]
        tab = np.concatenate([np.cos(ang), np.sin(ang)], axis=1).astype(np.float32)
        n = tab.shape[0] // 128
        return np.ascontiguousarray(tab.reshape(n, 128, 64).transpose(1, 0, 2))
    rp = rope_tab(np.arange(NTOK))
    rs = rope_tab(np.arange(core * NTOK - 128, (core + 1) * NTOK + 128))
    a = np.arange(128)[:, None]
    b = np.arange(128)[None, :]
    triL = (b <= a).astype(np.float32)
    triU = (a <= b).astype(np.float32)
    haloL = triL * (0.0 if core == 0 else 1.0)
    haloR = triU * (0.0 if core == NCORES - 1 else 1.0)
    masks = np.ascontiguousarray(np.stack([triL, triU, haloL, haloR], axis=1)).astype(np.float32)
    return rp, rs, masks


def _hy_consts(L, cg):
    R = L // 128
    P1 = 2 * R
    t2 = np.arange(128)
    k2 = np.arange(128)
    ang = 2 * np.pi * np.outer(t2, k2 + 0.5) / 256
    Gf = np.concatenate([np.cos(ang), -np.sin(ang)], axis=1)
    t1 = np.arange(R)
    eye = np.eye(cg)
    S2 = []
    for h in range(2):
        k1 = h * R + np.arange(R)
        a1 = 2 * np.pi * np.outer(t1, k1) / P1
        sg = ((-1.0) ** k1)[None, :]
        for sgn in (np.ones_like(sg), sg):
            S2.append(np.kron(eye, np.cos(a1) * sgn))
            S2.append(np.kron(eye, -np.sin(a1) * sgn))
            S2.append(np.kron(eye, np.sin(a1) * sgn))
    S2 = np.stack(S2, axis=1)
    t1o = np.arange(-1, R)
    NI = 2 * cg * (R + 1)
    INV = []
    for h in range(2):
        k1 = h * R + np.arange(R)
        ai = 2 * np.pi * np.outer(k1, t1o) / P1
        IR = np.kron(eye, np.cos(ai) / P1)
        II = np.kron(eye, np.sin(ai) / P1)
        INV.append(np.concatenate([IR, II], axis=1))
        INV.append(np.concatenate([-II, IR], axis=1))
    INV = np.stack(INV, axis=1)
    tt = np.arange(256)
    ag = 2 * np.pi * np.outer(k2 + 0.5, tt) / 256
    C = (2.0 / 256) * np.cos(ag)
    S = -(2.0 / 256) * np.sin(ag)
    S1I = np.stack([C[:, :128], S[:, :128], C[:, 128:], S[:, 128:]], axis=1)
    bands = np.linspace(1e-4, 15, 16, dtype=np.float32)
    def feats(pos):
        pos = pos.astype(np.float32)
        t = pos / np.float32(L - 1)
        a = (np.float32(2.0 * math.pi / L) * pos[:, None]) * bands[None, :]
        return np.concatenate([t[:, None], np.cos(a), -np.sin(a)], axis=1).astype(np.float32).T
    p = np.arange(L)
    zf = feats(p)
    zr = feats((L - p).astype(np.float64))
    negt = np.stack([-(p / (L - 1.0)), -((L - p) / (L - 1.0))], axis=0).astype(np.float32)
    negt = np.ascontiguousarray(negt.reshape(2, R, 128).transpose(2, 0, 1))
    f32 = lambda a: np.ascontiguousarray(a, dtype=np.float32)
    dl = np.abs(np.linspace(math.log(1e-2) / 1.5, math.log(1e-2) / 0.3, 512)).astype(np.float64)
    l1v = np.arange(R, dtype=np.float64)
    l2v = np.arange(128, dtype=np.float64)
    A0 = np.exp(-dl[:, None] * (128.0 * l1v[None, :]) / (L - 1.0))
    A1 = np.exp(-dl[:, None] * (L - 128.0 * l1v[None, :]) / (L - 1.0))
    B0 = np.exp(-l2v[:, None] * dl[None, :] / (L - 1.0))
    B1 = np.exp(+l2v[:, None] * dl[None, :] / (L - 1.0))
    Atab = f32(np.stack([A0.reshape(-1), A1.reshape(-1)], axis=0))
    Btab = f32(np.stack([B0, B1], axis=1))
    if cg == 1:
        return dict(Gf=f32(Gf), S2=f32(S2), INV=f32(INV), S1I=f32(S1I), zf=f32(np.stack([zf, zr], 0)), negt=negt, Atab=Atab, Btab=Btab)
    return dict(Gf=f32(Gf), S2=f32(S2), INV=f32(INV), S1I=f32(S1I), zf=f32(np.stack([zf, zr], 0)), negt=negt)


_NC_CACHE = {}


def kernel(**inputs):
    f = lambda k: np.ascontiguousarray(np.asarray(inputs[k], dtype=np.float32))
    x_prompt = f("x_prompt")
    x_sample = f("x_sample")[0]
    if "nc" not in _NC_CACHE:
        _NC_CACHE["nc"] = build_program()
    nc = _NC_CACHE["nc"]
    rep = lambda v, n=128: np.ascontiguousarray(np.broadcast_to(v.reshape(1, -1), (n, v.size)))
    common = {
        "w_in": f("w_in")[0], "w_hy": f("w_hy_out")[0], "w_at": f("w_at_out")[0], "w_o": f("w_o")[0],
        "w_gate": f("w_gate")[0], "w_up": f("w_up")[0], "w_down": f("w_down")[0],
        "g1_bc": rep(f("attn_norm_w")[0]), "g2_bc": rep(f("ffn_norm_w")[0]),
        "qw_bc": rep(np.tile(f("q_norm_w")[0], 8)), "kw_bc": rep(np.tile(f("k_norm_w")[0], 2)),
        "sink_bc": rep(f("attn_sink")[0]),
        "ident": np.eye(128, dtype=np.float32),
        "xs_all": x_sample,
        "w_in_hy": np.ascontiguousarray(f("w_in")[0][:, :1536]),
        "cw_all": np.ascontiguousarray(np.concatenate([f("hyena_conv_w")[0], f("hyena_conv_b")], axis=0).T),
        "fw1": f("filt_w1")[0], "fw2": f("filt_w2")[0], "fw3": f("filt_w3")[0],
        "fb": np.ascontiguousarray(np.stack([f("filt_b1")[0], f("filt_b2")[0], f("filt_freq")[0]], axis=1)),
        "hb_bc": rep(f("hyena_bias")[0].reshape(-1)),
        "absd_bc": rep(np.abs(np.linspace(math.log(1e-2) / 1.5, math.log(1e-2) / 0.3, 512, dtype=np.float32))),
    }
    for tag, (L_, cg_) in (("p", (2048, 8)), ("s", (16384, 1))):
        for k_, v_ in _hy_consts(L_, cg_).items():
            common[k_ + "_" + tag] = v_
    xs_pad = np.concatenate([np.zeros((128, D), np.float32), x_sample, np.zeros((128, D), np.float32)], axis=0)
    in_maps = []
    for c in range(NCORES):
        rp, rs, masks = _host_consts(c)
        m = dict(common)
        m["xp"] = x_prompt[c]
        m["xs"] = np.ascontiguousarray(xs_pad[c * NTOK:(c + 1) * NTOK + 256])
        m["rope_p"] = rp
        m["rope_s"] = rs
        m["masks"] = masks
        R_, P1_ = 128, 256
        t1o = np.arange(16 * c - 1, 16 * c + 16)
        invo = []
        for h in range(2):
            k1 = h * R_ + np.arange(R_)
            ai = 2 * np.pi * np.outer(k1, t1o) / P1_
            IR = np.cos(ai) / P1_
            II = np.sin(ai) / P1_
            invo.append(np.concatenate([IR, II], axis=1))
            invo.append(np.concatenate([-II, IR], axis=1))
        m["INVo_s"] = np.ascontiguousarray(np.stack(invo, axis=1), dtype=np.float32)
        in_maps.append(m)
    res = run_bass_kernel_spmd(nc, in_maps, core_ids=list(range(NCORES)))
    y_p = np.stack([np.asarray(res.results[c]["yp"], dtype=np.float32) for c in range(NCORES)], axis=0)
    y_s = np.concatenate([np.asarray(res.results[c]["ys"], dtype=np.float32) for c in range(NCORES)], axis=0)[None]
    return (y_p, y_s)
```

```python
import math
import numpy as np
import concourse.bass as bass
import concourse.mybir as mybir
from concourse.bass_utils import run_bass_kernel_spmd
from contextlib import ExitStack

F32 = mybir.dt.float32
BF16 = mybir.dt.bfloat16
ALU = mybir.AluOpType
AF = mybir.ActivationFunctionType
AX = mybir.AxisListType

D = 1024
HW = 512
NTOK = 2048
INW = 4352
FFN = 2816
EPS = 1e-6
NCORES = 8
ENABLE_HYENA = False
import os
KSTOP = int(os.environ.get('KSTOP', '9'))
KGROUPS = int(os.environ.get('KGROUPS', '2'))
KATT = int(os.environ.get('KATT', '9'))
KHY = int(os.environ.get('KHY', '3'))
KCORE_T10 = 0
RELAXED = tuple(os.environ.get('KRELAX', 'pe').split(','))


class Buf:
    __slots__ = ("w", "r")

    def __init__(self):
        self.w = None
        self.r = {}


class FW:
    def __init__(self, nc, sems, dma_sems):
        self.nc = nc
        self.eng = {"pe": nc.tensor, "act": nc.scalar, "dve": nc.vector, "pool": nc.gpsimd, "sp": nc.sync}
        self.sem = sems
        self.dsem = dma_sems
        self.dcnt = [0] * len(dma_sems)
        self.dnext = 0
        self.dnext_sw = 0
        self.cnt = {e: 0 for e in sems}
        self.ops = {e: [] for e in self.eng}
        self.known = {e: {} for e in self.eng}

    def _deps(self, reads, writes):
        deps = {}

        def add(k, v):
            if deps.get(k, 0) < v:
                deps[k] = v
        for b in reads:
            if b.w is not None:
                add(*b.w)
        for b in writes:
            if b.w is not None:
                add(*b.w)
            for k, v in b.r.items():
                add(k, v)
        return deps

    def _semobj(self, k):
        return self.sem[k] if isinstance(k, str) else self.dsem[k]

    def _wait(self, e, deps, skip_pe=True):
        for k, v in deps.items():
            if skip_pe and k == e and e in RELAXED:
                continue
            if self.known[e].get(k, 0) >= v:
                continue
            self.known[e][k] = v
            so = self._semobj(k)
            eo = self.eng[e]
            self.ops[e].append(lambda eo=eo, so=so, v=v: eo.wait_ge(so, v))

    def op(self, e, fn, reads=(), writes=()):
        self._wait(e, self._deps(reads, writes))
        self.cnt[e] += 1
        tok = (e, self.cnt[e])
        so = self.sem[e]
        self.ops[e].append(lambda fn=fn, so=so: fn().then_inc(so, 1))
        for b in writes:
            b.w = tok
            b.r = {}
        for b in reads:
            if b.r.get(e, 0) < tok[1]:
                b.r[e] = tok[1]

    def dma(self, q, fn, reads=(), writes=()):
        self._wait(q, self._deps(reads, writes))
        if q == "pool":
            i = self.dnext_sw
            self.dnext_sw = (self.dnext_sw + 1) % 8
        else:
            i = 8 + self.dnext
            self.dnext = (self.dnext + 1) % (len(self.dsem) - 8)
        if self.dcnt[i] > 0:
            self._wait(q, {i: self.dcnt[i]})
        self.dcnt[i] += 16
        tok = (i, self.dcnt[i])
        so = self.dsem[i]
        self.ops[q].append(lambda fn=fn, so=so: fn().then_inc(so, 16))
        for b in writes:
            b.w = tok
            b.r = {}
        for b in reads:
            if b.r.get(i, 0) < tok[1]:
                b.r[i] = tok[1]

    def barrier(self):
        deps = {k: v for k, v in self.cnt.items() if v > 0}
        for i, v in enumerate(self.dcnt):
            if v > 0:
                deps[i] = v
        for e in self.eng:
            self._wait(e, deps, skip_pe=False)

    def replay(self, block):
        fw = self

        @block.tensor
        def _(e):
            for f in fw.ops["pe"]:
                f()

        @block.scalar
        def _(e):
            for f in fw.ops["act"]:
                f()

        @block.vector
        def _(e):
            for f in fw.ops["dve"]:
                f()

        @block.gpsimd
        def _(e):
            for f in fw.ops["pool"]:
                f()

        @block.sync
        def _(e):
            for f in fw.ops["sp"]:
                f()


class Rot:
    def __init__(self, n, mk):
        self.items = [(mk(), Buf()) for _ in range(n)]
        self.i = 0

    def next(self):
        it = self.items[self.i % len(self.items)]
        self.i += 1
        return it


class Arena:
    def __init__(self, ap, ncols):
        self.ap = ap
        self.n = ncols
        self.off = 0

    def alloc(self, cols, dt=BF16):
        nb = cols * (2 if dt == F32 else 1)
        nb = (nb + 31) // 32 * 32
        assert self.off + nb <= self.n, ("arena overflow", self.off, nb, self.n)
        v = self.ap[:, self.off:self.off + nb]
        self.off += nb
        if dt == F32:
            v = v.bitcast(F32)[:, 0:cols]
        else:
            v = v[:, 0:cols]
        return v

    def mark(self):
        return self.off

    def reset(self, m):
        self.off = m


def build_program():
    nc = bass.Bass("TRN2", target_bir_lowering=False)

    def din(name, shape):
        return nc.dram_tensor(name, list(shape), F32, kind="ExternalInput").ap()

    xp = din("xp", [NTOK, D])
    xs = din("xs", [NTOK + 256, D])
    w_in = din("w_in", [D, INW])
    w_hy = din("w_hy", [HW, D])
    w_at = din("w_at", [HW, D])
    w_o = din("w_o", [D, D])
    w_gate = din("w_gate", [D, FFN])
    w_up = din("w_up", [D, FFN])
    w_down = din("w_down", [FFN, D])
    g1_bc = din("g1_bc", [128, D])
    g2_bc = din("g2_bc", [128, D])
    qw_bc = din("qw_bc", [128, 512])
    kw_bc = din("kw_bc", [128, 128])
    sink_bc = din("sink_bc", [128, 8])
    rope_p = din("rope_p", [128, 16, 64])
    rope_s = din("rope_s", [128, 18, 64])
    masks = din("masks", [128, 4, 128])
    ident_d = din("ident", [128, 128])
    xs_all = din("xs_all", [16384, D])
    w_in_hy = din("w_in_hy", [D, 1536])
    cw_all = din("cw_all", [1536, 4])
    fw1 = din("fw1", [33, 64])
    fw2 = din("fw2", [64, 64])
    fw3 = din("fw3", [64, 2048])
    fb = din("fb", [64, 3])
    hb_bc = din("hb_bc", [128, 1024])
    absd_bc = din("absd_bc", [128, 512])
    HC = {}
    for tag_, (L_, cg_) in (("p", (2048, 8)), ("s", (16384, 1))):
        R_ = L_ // 128
        NI_ = 2 * cg_ * (R_ + 1)
        HC[tag_] = dict(Gf=din("Gf_" + tag_, [128, 256]), S2=din("S2_" + tag_, [128, 12, 128]), INV=din("INV_" + tag_, [128, 4, NI_]),
                        S1I=din("S1I_" + tag_, [128, 4, 128]), zf=din("zf_" + tag_, [2, 33, L_]), negt=din("negt_" + tag_, [128, 2, R_]))
    HC["s"]["INVo"] = din("INVo_s", [128, 4, 34])
    HC["s"]["Atab"] = din("Atab_s", [2, 512 * 128])
    HC["s"]["Btab"] = din("Btab_s", [128, 2, 512])
    yp = nc.dram_tensor("yp", [NTOK, D], F32, kind="ExternalOutput").ap()
    ys = nc.dram_tensor("ys", [NTOK, D], F32, kind="ExternalOutput").ap()

    with ExitStack() as st:
        E = st.enter_context
        arena_t = E(nc.sbuf_tensor("arena", [128, 106368], BF16))
        psb = [E(nc.psum_tensor("ps%d" % i, [128, 512], F32)) for i in range(8)]
        sems = {k: E(nc.semaphore("s_" + k)) for k in ("pe", "act", "dve", "pool")}
        dsems = [E(nc.semaphore("d%d" % i)) for i in range(40)]
        block = E(nc.Block())
        fw = FW(nc, sems, dsems)
        ar = Arena(arena_t[:, :], 106368)
        V = nc.vector
        A = nc.scalar
        P = nc.tensor
        G = nc.gpsimd

        psbuf = [Buf() for _ in range(8)]
        psi = [0]

        def ps():
            i = psi[0]
            psi[0] = (i + 1) % 8
            return psb[i][:, :], psbuf[i]

        ident = ar.alloc(128)
        b_const = Buf()
        g1 = ar.alloc(D, F32)
        g2 = ar.alloc(D, F32)
        qw = ar.alloc(512, F32)
        kw = ar.alloc(128, F32)
        esink = ar.alloc(8, F32)
        ropep = ar.alloc(16 * 64, F32)
        ropes = ar.alloc(18 * 64, F32)
        msk = ar.alloc(4 * 128)
        ones_bf = ar.alloc(128)
        epsb = ar.alloc(1, F32)
        for dst, src in ((ident, ident_d), (msk, masks.rearrange("p a b -> p (a b)"))):
            fw.dma("pool", lambda dst=dst, src=src: G.dma_start(out=dst, in_=src), writes=[b_const])
        for dst, src in ((g1, g1_bc), (g2, g2_bc), (qw, qw_bc), (kw, kw_bc), (esink, sink_bc),
                         (ropep, rope_p.rearrange("p a b -> p (a b)")), (ropes, rope_s.rearrange("p a b -> p (a b)"))):
            fw.dma("sp", lambda dst=dst, src=src: nc.sync.dma_start(out=dst, in_=src), writes=[b_const])
        fw.op("dve", lambda: V.memset(ones_bf, 1.0), writes=[b_const])
        fw.op("dve", lambda: V.memset(epsb, EPS), writes=[b_const])
        fw.op("act", lambda: A.activation(out=esink, in_=esink, func=AF.Exp), reads=[b_const], writes=[b_const])
        fw.barrier()

        NWB = 3
        wbuf = []
        wbb = [Buf() for _ in range(NWB)]
        wbi = [0]

        def wload(src3):
            i = wbi[0]
            wbi[0] = (i + 1) % NWB
            kch, ncols = src3.shape[1], src3.shape[2]
            dst = wbuf[i][:, 0:kch * ncols].rearrange("p (k c) -> p k c", k=kch)
            fw.dma("pool", lambda dst=dst, src3=src3: G.dma_start(out=dst, in_=src3), writes=[wbb[i]])
            return dst, wbb[i]

        def wsrc(w, kch, c0, ncols, k0=0):
            return w.rearrange("(k p) c -> p k c", p=128)[:, k0:k0 + kch, c0:c0 + ncols]

        xt = [ar.alloc(D, F32) for _ in range(2)]
        xtb = [Buf() for _ in range(2)]
        junkR = Rot(2, lambda: ar.alloc(D))
        hbR = Rot(2, lambda: ar.alloc(D))
        smallR = Rot(4, lambda: ar.alloc(16, F32))

        def rstd_of(src_ap, ncol, groups, reads):
            junk, bjunk = junkR.next()
            sm_, bsmall = smallR.next()
            out_ap = sm_[:, 0:groups]
            jv = junk[:, 0:groups * ncol]
            fw.op("act", lambda: A.activation(out=jv, in_=src_ap, func=AF.Square), reads=reads, writes=[bjunk])
            fw.op("dve", lambda: V.tensor_reduce(out=out_ap, in_=jv.rearrange("p (g c) -> p g c", g=groups),
                                                 axis=AX.X, op=ALU.add), reads=[bjunk], writes=[bsmall])
            fw.op("act", lambda: A.activation(out=out_ap, in_=out_ap, func=AF.Sqrt, bias=epsb, scale=1.0 / ncol),
                  reads=[bsmall, b_const], writes=[bsmall])
            fw.op("dve", lambda: V.reciprocal(out=out_ap, in_=out_ap), reads=[bsmall], writes=[bsmall])
            return out_ap, bsmall

        def norm_transpose(src_fn, tiles, hT, bhT, gbc):
            for n, (ts, td) in enumerate(tiles):
                x_t, bx = xt[n % 2], xtb[n % 2]
                fw.dma("sp", lambda x_t=x_t, ts=ts: nc.sync.dma_start(out=x_t, in_=src_fn(ts)), writes=[bx])
                rs, bsmall = rstd_of(x_t, D, 1, [bx])
                hb, bhb = hbR.next()
                fw.op("dve", lambda x_t=x_t, rs=rs, hb=hb: V.scalar_tensor_tensor(out=hb, in0=x_t, scalar=rs, in1=gbc,
                                                                                 op0=ALU.mult, op1=ALU.mult),
                      reads=[bx, bsmall, b_const], writes=[bhb])
                p, bp = ps()
                pv = p.bitcast(BF16)
                for k in range(8):
                    fw.op("pe", lambda pv=pv, k=k, hb=hb: P.transpose(out=pv[:, k * 128:(k + 1) * 128],
                                                                        in_=hb[:, k * 128:(k + 1) * 128], identity=ident),
                          reads=[bhb, b_const], writes=[bp])
                fw.op("act", lambda pv=pv, td=td: A.copy(out=hT[:, :, td * 128:(td + 1) * 128],
                                                          in_=pv.rearrange("p (k t) -> p k t", k=8)),
                      reads=[bp], writes=[bhT])

        def run_group(xsrc_fn, tiles_in, own0, ntl, rope, ydst, halo_masks, hyT, bhyT):
            m0 = ar.mark()
            NT_ALL = tiles_in * 128
            hT = ar.alloc(8 * NT_ALL).rearrange("p (k t) -> p k t", k=8)
            bhT = Buf()
            norm_transpose(xsrc_fn, [(t, t) for t in range(tiles_in)], hT, bhT, g1)

            if KSTOP <= 1:
                fw.barrier(); ar.reset(m0); return
            yat = ar.alloc(4 * NTOK).rearrange("p (j t) -> p j t", j=4)
            byat = Buf()
            m1 = ar.mark()
            qT = ar.alloc(4 * NTOK).rearrange("p (j t) -> p j t", j=4)
            kT = [[ar.alloc(NT_ALL) for _ in range(2)] for _ in range(2)]
            v2 = ar.alloc(tiles_in * 256).rearrange("p (t c) -> p t c", t=tiles_in)
            bqT, bkT, bv2 = Buf(), Buf(), Buf()
            qfR = Rot(2, lambda: ar.alloc(768, F32))
            qnR = Rot(2, lambda: ar.alloc(640, F32))
            qrR = Rot(2, lambda: ar.alloc(1024))
            for (qr_, bqr_) in qrR.items:
                fw.op("dve", lambda qr_=qr_: V.memset(qr_, 0.0), writes=[bqr_])
            t1R = Rot(2, lambda: ar.alloc(320, F32))
            t2R = Rot(2, lambda: ar.alloc(320, F32))
            wq1, bwq1 = wload(wsrc(w_in, 8, 1536, 512))
            wq2, bwq2 = wload(wsrc(w_in, 8, 2048, 256))
            def qkv_tile(t, qf, bqf, qn, bqn, qr, bqr, t1, bt1, t2, bt2):
                own = own0 <= t < own0 + ntl
                pa, bpa = ps()
                pb, bpb = ps()
                if own:
                    for k in range(8):
                        fw.op("pe", lambda pa=pa, k=k, t=t: P.matmul(pa, lhsT=hT[:, k, t * 128:(t + 1) * 128], rhs=wq1[:, k, :],
                                                                    start=(k == 0), stop=(k == 7)),
                              reads=[bhT, bwq1], writes=[bpa])
                for k in range(8):
                    fw.op("pe", lambda pb=pb, k=k, t=t: P.matmul(pb[:, 0:256], lhsT=hT[:, k, t * 128:(t + 1) * 128], rhs=wq2[:, k, :],
                                                                start=(k == 0), stop=(k == 7)),
                          reads=[bhT, bwq2], writes=[bpb])
                if own:
                    fw.op("act", lambda pa=pa: A.copy(out=qf[:, 0:512], in_=pa), reads=[bpa], writes=[bqf])
                fw.op("act", lambda pb=pb: A.copy(out=qf[:, 512:768], in_=pb[:, 0:256]), reads=[bpb], writes=[bqf])
                for g in range(2):
                    for dup in range(2):
                        fw.op("pool", lambda t=t, g=g, dup=dup: G.tensor_copy(out=v2[:, t, g * 128 + dup * 64:g * 128 + dup * 64 + 64],
                                                                             in_=qf[:, 640 + g * 64:640 + g * 64 + 64]),
                              reads=[bqf], writes=[bv2])
                c0 = 0 if own else 512
                nh = 10 if own else 2
                h0 = c0 // 64
                src = qf[:, c0:640]
                rs, bsmall = rstd_of(src, 64, nh, [bqf])
                qnv = qn[:, c0:640].rearrange("p (h c) -> p h c", c=64)
                fw.op("dve", lambda src=src, rs=rs, nh=nh, qnv=qnv: V.tensor_tensor(
                    out=qnv, in0=src.rearrange("p (h c) -> p h c", c=64),
                    in1=rs.unsqueeze(2).to_broadcast([128, nh, 64]), op=ALU.mult),
                    reads=[bqf, bsmall], writes=[bqn])
                if own:
                    fw.op("dve", lambda: V.tensor_tensor(out=qn[:, 0:512], in0=qn[:, 0:512], in1=qw, op=ALU.mult),
                          reads=[bqn, b_const], writes=[bqn])
                fw.op("dve", lambda: V.tensor_tensor(out=qn[:, 512:640], in0=qn[:, 512:640], in1=kw, op=ALU.mult),
                      reads=[bqn, b_const], writes=[bqn])
                cs = rope[:, t * 64:t * 64 + 32].unsqueeze(1).to_broadcast([128, nh, 32])
                sn = rope[:, t * 64 + 32:t * 64 + 64].unsqueeze(1).to_broadcast([128, nh, 32])
                a_ = qnv[:, :, 0:32]
                b_ = qnv[:, :, 32:64]
                t1v = t1[:, 0:nh * 32].rearrange("p (h c) -> p h c", c=32)
                t2v = t2[:, 0:nh * 32].rearrange("p (h c) -> p h c", c=32)
                if own:
                    dq = qr[:, 0:512].rearrange("p (h c) -> p h c", c=64)
                fw.op("dve", lambda a_=a_, cs=cs, t1v=t1v: V.tensor_tensor(out=t1v, in0=a_, in1=cs, op=ALU.mult),
                      reads=[bqn, b_const], writes=[bt1])
                fw.op("pool", lambda b_=b_, sn=sn, t2v=t2v: G.tensor_tensor(out=t2v, in0=b_, in1=sn, op=ALU.mult),
                      reads=[bqn, b_const], writes=[bt2])
                nq = nh - 2
                if own:
                    fw.op("dve", lambda t1v=t1v, t2v=t2v, dq=dq, nq=nq: V.tensor_tensor(out=dq[:, :, 0:32], in0=t1v[:, 0:nq, :], in1=t2v[:, 0:nq, :], op=ALU.subtract),
                          reads=[bt1, bt2], writes=[bqr])
                for g in range(2):
                    for dup in range(2):
                        o0 = 512 + g * 256 + dup * 192
                        fw.op("dve", lambda t1v=t1v, t2v=t2v, o0=o0, g=g, nq=nq: V.tensor_tensor(
                            out=qr[:, o0:o0 + 32], in0=t1v[:, nq + g, :], in1=t2v[:, nq + g, :], op=ALU.subtract),
                            reads=[bt1, bt2], writes=[bqr])
                fw.op("dve", lambda b_=b_, cs=cs, t1v=t1v: V.tensor_tensor(out=t1v, in0=b_, in1=cs, op=ALU.mult),
                      reads=[bqn, b_const, bqr], writes=[bt1])
                fw.op("pool", lambda a_=a_, sn=sn, t2v=t2v: G.tensor_tensor(out=t2v, in0=a_, in1=sn, op=ALU.mult),
                      reads=[bqn, b_const, bqr], writes=[bt2])
                if own:
                    fw.op("dve", lambda t1v=t1v, t2v=t2v, dq=dq, nq=nq: V.tensor_tensor(out=dq[:, :, 32:64], in0=t1v[:, 0:nq, :], in1=t2v[:, 0:nq, :], op=ALU.add),
                          reads=[bt1, bt2], writes=[bqr])
                for g in range(2):
                    for dup in range(2):
                        o0 = 512 + g * 256 + dup * 192 + 32
                        fw.op("dve", lambda t1v=t1v, t2v=t2v, o0=o0, g=g, nq=nq: V.tensor_tensor(
                            out=qr[:, o0:o0 + 32], in0=t1v[:, nq + g, :], in1=t2v[:, nq + g, :], op=ALU.add),
                            reads=[bt1, bt2], writes=[bqr])
                p, bp = ps()
                pv = p.bitcast(BF16)
                chunks = list(range(4, 8)) + (list(range(4)) if own else [])
                for c in chunks:
                    fw.op("pe", lambda pv=pv, c=c: P.transpose(out=pv[:, c * 128:(c + 1) * 128], in_=qr[:, c * 128:(c + 1) * 128], identity=ident),
                          reads=[bqr, b_const], writes=[bp])
                if own:
                    to = t - own0
                    fw.op("act", lambda pv=pv, to=to: A.copy(out=qT[:, :, to * 128:(to + 1) * 128],
                                                              in_=pv[:, 0:512].rearrange("p (j t) -> p j t", j=4)),
                          reads=[bp], writes=[bqT])
                for g in range(2):
                    for par in range(2):
                        cc = 4 + 2 * g + par
                        fw.op("act", lambda pv=pv, g=g, par=par, t=t, cc=cc: A.copy(out=kT[g][par][:, t * 128:(t + 1) * 128], in_=pv[:, cc * 128:(cc + 1) * 128]),
                              reads=[bp], writes=[bkT])

            for t in range(tiles_in):
                (qf, bqf), (qn, bqn), (qr, bqr), (t1, bt1), (t2, bt2) = qfR.next(), qnR.next(), qrR.next(), t1R.next(), t2R.next()
                qkv_tile(t, qf, bqf, qn, bqn, qr, bqr, t1, bt1, t2, bt2)
            if KSTOP <= 2:
                fw.barrier(); ar.reset(m0); return
            pT = [ar.alloc(512) for _ in range(3)]
            bpT = [Buf() for _ in range(3)]
            rec = ar.alloc(512, F32)
            brec = Buf()
            pidx = 0
            for n in range(ntl):
                tq = own0 + n
                for g in range(2):
                    kbs = [kb for kb in (tq - 1, tq, tq + 1) if 0 <= kb < tiles_in]
                    po, bpo = ps()
                    pd, bpd = ps()
                    for ii, kb in enumerate(kbs):
                        pS, bpS = ps()
                        for hh in range(4):
                            h = 4 * g + hh
                            pbase = (h % 2) * 64
                            j = h // 2
                            fw.op("pe", lambda pS=pS, hh=hh, h=h, j=j, kb=kb, n=n, g=g: P.matmul(
                                pS[:, hh * 128:(hh + 1) * 128], lhsT=kT[g][h % 2][:, kb * 128:(kb + 1) * 128],
                                rhs=qT[:, j, n * 128:(n + 1) * 128], start=True, stop=True),
                                reads=[bkT, bqT], writes=[bpS])
                        pt, bpt = pT[pidx % 3], bpT[pidx % 3]
                        pidx += 1
                        fw.op("act", lambda pS=pS, pt=pt: A.activation(out=pt, in_=pS, func=AF.Exp, scale=0.125),
                              reads=[bpS], writes=[bpt])
                        mi = None
                        if kb == tq - 1:
                            mi = 2 if (halo_masks and kb < own0) else 0
                        elif kb == tq + 1:
                            mi = 3 if (halo_masks and kb >= own0 + ntl) else 1
                        if mi is not None and KATT >= 2:
                            mv = msk[:, mi * 128:(mi + 1) * 128].unsqueeze(1).to_broadcast([128, 4, 128])
                            fw.op("pool", lambda pt=pt, mv=mv: G.tensor_tensor(out=pt.rearrange("p (h q) -> p h q", h=4),
                                                                              in0=pt.rearrange("p (h q) -> p h q", h=4), in1=mv, op=ALU.mult),
                                  reads=[bpt, b_const], writes=[bpt])
                        if KATT < 3:
                            continue
                        fw.op("pe", lambda po=po, kb=kb, g=g, pt=pt, ii=ii, kbs=kbs: P.matmul(
                            po, lhsT=v2[:, kb, g * 128:(g + 1) * 128], rhs=pt, start=(ii == 0), stop=(ii == len(kbs) - 1)),
                            reads=[bv2, bpt], writes=[bpo])
                        fw.op("pe", lambda pd=pd, pt=pt, ii=ii, kbs=kbs: P.matmul(
                            pd, lhsT=ones_bf, rhs=pt, start=(ii == 0), stop=(ii == len(kbs) - 1)),
                            reads=[b_const, bpt], writes=[bpd])
                    if KATT < 4:
                        continue
                    es = esink[:, 4 * g:4 * g + 4].unsqueeze(2).to_broadcast([128, 4, 128])
                    fw.op("dve", lambda pd=pd, es=es: V.tensor_tensor(out=rec.rearrange("p (h q) -> p h q", h=4),
                                                                     in0=pd.rearrange("p (h q) -> p h q", h=4), in1=es, op=ALU.add),
                          reads=[bpd, b_const], writes=[brec])
                    fw.op("dve", lambda: V.reciprocal(out=rec, in_=rec), reads=[brec], writes=[brec])
                    for hh in range(4):
                        h = 4 * g + hh
                        pbase = (h % 2) * 64
                        j = h // 2
                        fw.op("dve", lambda po=po, hh=hh, pbase=pbase, j=j, n=n: V.tensor_tensor(
                            out=yat[pbase:pbase + 64, j, n * 128:(n + 1) * 128], in0=po[pbase:pbase + 64, hh * 128:(hh + 1) * 128],
                            in1=rec[pbase:pbase + 64, hh * 128:(hh + 1) * 128], op=ALU.mult),
                            reads=[bpo, brec], writes=[byat])

            if KSTOP <= 3:
                fw.barrier(); ar.reset(m0); return
            fw.barrier()
            ar.reset(m1)
            GT = ar.alloc(2 * 512).rearrange("p (c t) -> p c t", c=2)
            bGT = Buf()
            mg = ar.alloc(8 * 512).rearrange("p (c t) -> p c t", c=8)
            bmg = Buf()
            x1 = ar.alloc(4 * D, F32).rearrange("p (t d) -> p t d", t=4)
            bx1 = Buf()
            fT = ar.alloc(8 * 512).rearrange("p (k t) -> p k t", k=8)
            bfT = Buf()
            hf = ar.alloc(22 * 512).rearrange("p (c t) -> p c t", c=22)
            bhf = Buf()
            sg = [ar.alloc(512) for _ in range(2)]
            bsg = [Buf(), Buf()]
            tm = [ar.alloc(512, F32) for _ in range(2)]
            btm = [Buf(), Buf()]
            for b in range(ntl // 4):
                tok0 = (own0 + b * 4) * 128
                o0 = b * 512
                fw.dma("sp", lambda b=b: nc.sync.dma_start(out=x1, in_=xsrc_fn(own0 + b * 4, 4).rearrange("(t p) d -> p t d", p=128)),
                       writes=[bx1])
                for c in range(8):
                    for gi in range(2):
                        wt, bw = wload(wsrc(w_in, 8, 2304 + gi * 1024 + c * 128, 128))
                        p, bp = ps()
                        for k in range(8):
                            fw.op("pe", lambda p=p, k=k, wt=wt, tok0=tok0: P.matmul(p, lhsT=wt[:, k, :], rhs=hT[:, k, tok0:tok0 + 512],
                                                                                    start=(k == 0), stop=(k == 7)),
                                  reads=[bw, bhT], writes=[bp])
                        fw.op("act", lambda p=p, gi=gi: A.activation(out=GT[:, gi, :], in_=p, func=AF.Sigmoid), reads=[bp], writes=[bGT])
                    wa, bwa = wload(wsrc(w_hy, 4, c * 128, 128))
                    wb_, bwb = wload(wsrc(w_at, 4, c * 128, 128))
                    pa, bpa = ps()
                    pb, bpb = ps()
                    for k in range(4):
                        fw.op("pe", lambda pa=pa, k=k, wa=wa, o0=o0: P.matmul(pa, lhsT=wa[:, k, :], rhs=hyT[:, k, o0:o0 + 512],
                                                                              start=(k == 0), stop=(k == 3)),
                              reads=[bwa, bhyT], writes=[bpa])
                    for k in range(4):
                        fw.op("pe", lambda pb=pb, k=k, wb_=wb_, o0=o0: P.matmul(pb, lhsT=wb_[:, k, :], rhs=yat[:, k, o0:o0 + 512],
                                                                                start=(k == 0), stop=(k == 3)),
                              reads=[bwb, byat], writes=[bpb])
                    ta, bta = tm[0], btm[0]
                    tb, btb = tm[1], btm[1]
                    fw.op("dve", lambda pa=pa, c=c, ta=ta: V.tensor_tensor(out=ta, in0=pa, in1=GT[:, 0, :], op=ALU.mult),
                          reads=[bpa, bGT], writes=[bta])
                    fw.op("dve", lambda pb=pb, c=c, tb=tb: V.tensor_tensor(out=tb, in0=pb, in1=GT[:, 1, :], op=ALU.mult),
                          reads=[bpb, bGT], writes=[btb])
                    fw.op("pool", lambda c=c, ta=ta, tb=tb: G.tensor_tensor(out=mg[:, c, :], in0=ta, in1=tb, op=ALU.add),
                          reads=[bta, btb], writes=[bmg])
                for nh_ in range(2):
                    wt, bw = wload(wsrc(w_o, 8, nh_ * 512, 512))
                    for tt in range(4):
                        p, bp = ps()
                        for k in range(8):
                            fw.op("pe", lambda p=p, k=k, wt=wt, tt=tt: P.matmul(p, lhsT=mg[:, k, tt * 128:(tt + 1) * 128], rhs=wt[:, k, :],
                                                                                start=(k == 0), stop=(k == 7)),
                                  reads=[bw, bmg], writes=[bp])
                        fw.op("dve", lambda p=p, tt=tt, nh_=nh_: V.tensor_tensor(out=x1[:, tt, nh_ * 512:(nh_ + 1) * 512], in0=p,
                                                                                 in1=x1[:, tt, nh_ * 512:(nh_ + 1) * 512], op=ALU.add),
                              reads=[bp, bx1], writes=[bx1])
                for tt in range(4):
                    rs, bsmall = rstd_of(x1[:, tt, :], D, 1, [bx1])
                    hb, bhb = hbR.next()
                    fw.op("dve", lambda tt=tt, rs=rs, hb=hb: V.scalar_tensor_tensor(out=hb, in0=x1[:, tt, :], scalar=rs, in1=g2,
                                                                                   op0=ALU.mult, op1=ALU.mult),
                          reads=[bx1, bsmall, b_const], writes=[bhb])
                    p, bp = ps()
                    pv = p.bitcast(BF16)
                    for k in range(8):
                        fw.op("pe", lambda pv=pv, k=k, hb=hb: P.transpose(out=pv[:, k * 128:(k + 1) * 128], in_=hb[:, k * 128:(k + 1) * 128], identity=ident),
                              reads=[bhb, b_const], writes=[bp])
                    fw.op("act", lambda pv=pv, tt=tt: A.copy(out=fT[:, :, tt * 128:(tt + 1) * 128], in_=pv.rearrange("p (k t) -> p k t", k=8)),
                          reads=[bp], writes=[bfT])
                for c in range(22):
                    wg, bwg = wload(wsrc(w_gate, 8, c * 128, 128))
                    wu, bwu = wload(wsrc(w_up, 8, c * 128, 128))
                    pg, bpg = ps()
                    pu, bpu = ps()
                    for k in range(8):
                        fw.op("pe", lambda pg=pg, k=k, wg=wg: P.matmul(pg, lhsT=wg[:, k, :], rhs=fT[:, k, :], start=(k == 0), stop=(k == 7)),
                              reads=[bwg, bfT], writes=[bpg])
                    for k in range(8):
                        fw.op("pe", lambda pu=pu, k=k, wu=wu: P.matmul(pu, lhsT=wu[:, k, :], rhs=fT[:, k, :], start=(k == 0), stop=(k == 7)),
                              reads=[bwu, bfT], writes=[bpu])
                    s_, bs_ = sg[c % 2], bsg[c % 2]
                    fw.op("act", lambda pg=pg, s_=s_: A.activation(out=s_, in_=pg, func=AF.Silu), reads=[bpg], writes=[bs_])
                    fw.op("dve", lambda pu=pu, s_=s_, c=c: V.tensor_tensor(out=hf[:, c, :], in0=pu, in1=s_, op=ALU.mult),
                          reads=[bpu, bs_], writes=[bhf])
                for nh_ in range(2):
                    for k0_, kn_ in ((0, 8), (8, 8), (16, 6)):
                        wt, bw = wload(wsrc(w_down, kn_, nh_ * 512, 512, k0=k0_))
                        for tt in range(4):
                            p, bp = ps()
                            for k in range(kn_):
                                fw.op("pe", lambda p=p, k=k, wt=wt, tt=tt, k0_=k0_, kn_=kn_: P.matmul(
                                    p, lhsT=hf[:, k0_ + k, tt * 128:(tt + 1) * 128], rhs=wt[:, k, :], start=(k == 0), stop=(k == kn_ - 1)),
                                    reads=[bw, bhf], writes=[bp])
                            fw.op("dve", lambda p=p, tt=tt, nh_=nh_: V.tensor_tensor(out=x1[:, tt, nh_ * 512:(nh_ + 1) * 512], in0=p,
                                                                                     in1=x1[:, tt, nh_ * 512:(nh_ + 1) * 512], op=ALU.add),
                                  reads=[bp, bx1], writes=[bx1])
                fw.dma("act", lambda b=b: nc.scalar.dma_start(out=ydst[b * 512:(b + 1) * 512, :].rearrange("(t p) d -> p t d", p=128), in_=x1),
                       reads=[bx1])
            fw.barrier()
            ar.reset(m0)


        def hyena_phase(tag, xsrc_fn, L, cg, own_static, hyT, bhyT):
            C = HC[tag]
            R = L // 128
            GP = 16
            CP = GP * cg
            NI = 2 * cg * (R + 1)
            NPASS = 512 // CP
            NT1 = 512 // CP
            own_t10 = 0
            NIo = 2 * cg * 17
            uT_d = nc.dram_tensor("uT_" + tag, [1536, L + 2], BF16).ap()
            u2_d = nc.dram_tensor("u2_" + tag, [512, 2304], BF16).ap()
            h2_d = nc.dram_tensor("h2_" + tag, [2, 64, L], BF16).ap()
            buT, bh2d = Buf(), Buf()
            m0 = ar.mark()
            whyT = ar.alloc(12 * 8 * 128).rearrange("p (c k n) -> p c k n", c=12, k=8)
            bwhy = Buf()
            for c in range(12):
                fw.dma("pool", lambda c=c: G.dma_start(out=whyT[:, c], in_=w_in_hy.rearrange("(k p) n -> p k n", p=128)[:, :, c * 128:(c + 1) * 128]),
                       writes=[bwhy])
            hTb = [ar.alloc(8 * 512).rearrange("p (k t) -> p k t", k=8) for _ in range(2)]
            bhTb = [Buf(), Buf()]
            stgR = Rot(2, lambda: ar.alloc(12 * 512).rearrange("p (c t) -> p c t", c=12))
            stg, bstg = stgR.items[0]
            zt_ = ar.alloc(16)
            bzt = Buf()
            fw.op("dve", lambda: V.memset(zt_, 0.0), writes=[bzt])
            for col in (0, L + 1):
                fw.dma("sp", lambda col=col: nc.sync.dma_start(out=uT_d[:, col:col + 1].rearrange("(c p) o -> p (c o)", p=128), in_=zt_[:, 0:12], allow_slow_non_contiguous=True),
                       reads=[bzt], writes=[buT])
            for b in range(L // 512):
                hT_b, bh = hTb[b % 2], bhTb[b % 2]
                norm_transpose(xsrc_fn, [(b * 4 + i, i) for i in range(4)], hT_b, bh, g1)
                stg, bstg = stgR.next()
                for c in range(12):
                    p, bp = ps()
                    for k in range(8):
                        fw.op("pe", lambda p=p, c=c, k=k, hT_b=hT_b: P.matmul(p, lhsT=whyT[:, c, k, :], rhs=hT_b[:, k, :], start=(k == 0), stop=(k == 7)),
                              reads=[bwhy, bh], writes=[bp])
                    if c % 2 == 0:
                        fw.op("act", lambda p=p, c=c, stg=stg: A.copy(out=stg[:, c, :], in_=p), reads=[bp], writes=[bstg])
                    else:
                        fw.op("dve", lambda p=p, c=c, stg=stg: V.tensor_copy(out=stg[:, c, :], in_=p), reads=[bp], writes=[bstg])
                fw.dma("act", lambda b=b, stg=stg: nc.scalar.dma_start(out=uT_d[:, 1 + b * 512:1 + (b + 1) * 512].rearrange("(c p) t -> p c t", p=128), in_=stg),
                       reads=[bstg], writes=[buT])
            if not own_static:
                for b in range(5):
                    nt_ = 4 if b < 4 else 2
                    hT_b, bh = hTb[b % 2], bhTb[b % 2]
                    norm_transpose(lambda t: xs[t * 128:(t + 1) * 128, :], [(b * 4 + i, i) for i in range(nt_)], hT_b, bh, g1)
                    for c in range(4):
                        p, bp = ps()
                        for k in range(8):
                            fw.op("pe", lambda p=p, c=c, k=k, hT_b=hT_b, nt_=nt_: P.matmul(p[:, 0:nt_ * 128], lhsT=whyT[:, 4 + c, k, :], rhs=hT_b[:, k, 0:nt_ * 128],
                                                                                       start=(k == 0), stop=(k == 7)), reads=[bwhy, bh], writes=[bp])
                        fw.op("act", lambda p=p, c=c, stg=stg: A.copy(out=stg[:, c, :], in_=p), reads=[bp], writes=[bstg])
                    fw.dma("act", lambda b=b, nt_=nt_, stg=stg: nc.scalar.dma_start(out=u2_d[:, b * 512:b * 512 + nt_ * 128].rearrange("(c p) t -> p c t", p=128),
                                                                       in_=stg[:, 0:4, 0:nt_ * 128]), reads=[bstg], writes=[buT])
            fw.barrier()
            ar.reset(m0)

            w1s = ar.alloc(128, F32)
            w2s = ar.alloc(128, F32)
            fbt = ar.alloc(4, F32)
            frb = ar.alloc(2, F32)
            bmw = Buf()
            fw.op("dve", lambda: V.memset(w1s, 0.0), writes=[bmw])
            fw.op("dve", lambda: V.memset(w2s, 0.0), writes=[bmw])
            fw.dma("sp", lambda: nc.sync.dma_start(out=w1s[0:33, 0:64], in_=fw1), writes=[bmw])
            fw.dma("sp", lambda: nc.sync.dma_start(out=w1s[33:66, 64:128], in_=fw1), writes=[bmw])
            fw.dma("sp", lambda: nc.sync.dma_start(out=w2s[0:64, 0:64], in_=fw2), writes=[bmw])
            fw.dma("sp", lambda: nc.sync.dma_start(out=w2s[64:128, 64:128], in_=fw2), writes=[bmw])
            fw.dma("sp", lambda: nc.sync.dma_start(out=fbt[0:64, 0:3], in_=fb), writes=[bmw])
            fw.dma("sp", lambda: nc.sync.dma_start(out=fbt[64:128, 0:3], in_=fb), writes=[bmw])
            fw.op("dve", lambda: V.tensor_scalar(frb[:, 0:2], fbt[:, 0:2], fbt[:, 2:3], None, op0=ALU.mult), reads=[bmw], writes=[bmw])
            ztR = Rot(2, lambda: ar.alloc(512, F32))
            a1R = Rot(2, lambda: ar.alloc(512, F32))
            mkR = Rot(2, lambda: ar.alloc(512, F32))
            h1R = Rot(2, lambda: ar.alloc(512, F32))
            h2R = Rot(2, lambda: ar.alloc(512))

            def sin_layer(p, bp, col, out_ap, bout):
                a1, ba1 = a1R.next()
                mk, bmk = mkR.next()
                fw.op("dve", lambda: V.tensor_scalar(a1, p, fbt[:, 2:3], frb[:, col:col + 1], op0=ALU.mult, op1=ALU.add),
                      reads=[bp, bmw], writes=[ba1])
                for _ in range(1):
                    fw.op("dve", lambda: V.tensor_scalar(mk, a1, math.pi, -2 * math.pi, op0=ALU.is_gt, op1=ALU.mult), reads=[ba1], writes=[bmk])
                    fw.op("dve", lambda: V.tensor_tensor(a1, a1, mk, op=ALU.add), reads=[ba1, bmk], writes=[ba1])
                    fw.op("dve", lambda: V.tensor_scalar(mk, a1, -math.pi, 2 * math.pi, op0=ALU.is_lt, op1=ALU.mult), reads=[ba1], writes=[bmk])
                    fw.op("dve", lambda: V.tensor_tensor(a1, a1, mk, op=ALU.add), reads=[ba1, bmk], writes=[ba1])
                fw.op("act", lambda: A.activation(out=out_ap, in_=a1, func=AF.Sin), reads=[ba1], writes=[bout])

            def trunk_block(blk):
                z_, bz_ = ztR.next()
                hh_, bhh_ = h2R.next()
                h1, bh1 = h1R.next()
                for pn in range(2):
                    fw.dma("sp", lambda pn=pn: nc.sync.dma_start(out=z_[pn * 33:(pn + 1) * 33, :], in_=C["zf"][pn, :, blk * 512:(blk + 1) * 512]), writes=[bz_])
                p, bp = ps()
                fw.op("pe", lambda: P.matmul(p, lhsT=w1s[0:66, :], rhs=z_[0:66, :], start=True, stop=True), reads=[bmw, bz_], writes=[bp])
                sin_layer(p, bp, 0, h1, bh1)
                p2, bp2 = ps()
                fw.op("pe", lambda: P.matmul(p2, lhsT=w2s, rhs=h1, start=True, stop=True), reads=[bmw, bh1], writes=[bp2])
                sin_layer(p2, bp2, 1, hh_, bhh_)
                for pn in range(2):
                    fw.dma("act", lambda pn=pn: nc.scalar.dma_start(out=h2_d[pn, :, blk * 512:(blk + 1) * 512], in_=hh_[pn * 64:(pn + 1) * 64, :]),
                           reads=[bhh_], writes=[bh2d])

            for blk in range(L // 512):
                trunk_block(blk)
            fw.barrier()
            ar.reset(m0)

            if not own_static:
                zd = nc.dram_tensor("zd_" + tag, [2, 128, 512, R], BF16).ap()
                bzd = Buf()
                stT = ar.alloc(128 * R).rearrange("p (c t) -> p c t", c=128)
                bstT = Buf()
                UppR = Rot(2, lambda: ar.alloc(2050))
                ucfR = Rot(2, lambda: ar.alloc(2048, F32))
                ucbR = Rot(2, lambda: ar.alloc(2048))
                cwt_p = ar.alloc(4, F32)
                bcwt_p = Buf()

                def pre_piece(row0, pc):
                    Upp, bUpp = UppR.next()
                    ucf_p, bucf_p = ucfR.next()
                    ucb_p, bucb_p = ucbR.next()
                    s0 = pc * 2048
                    fw.dma("sp", lambda: nc.sync.dma_start(out=Upp, in_=uT_d[row0:row0 + 128, s0:s0 + 2050]), reads=[buT], writes=[bUpp])
                    fw.op("act", lambda: A.activation(out=ucf_p, in_=Upp[:, 1:2049], func=AF.Identity, bias=cwt_p[:, 3:4], scale=cwt_p[:, 1:2]),
                          reads=[bUpp, bcwt_p], writes=[bucf_p])
                    fw.op("dve", lambda: V.scalar_tensor_tensor(out=ucf_p, in0=Upp[:, 0:2048], scalar=cwt_p[:, 0:1], in1=ucf_p, op0=ALU.mult, op1=ALU.add),
                          reads=[bUpp, bcwt_p, bucf_p], writes=[bucf_p])
                    fw.op("dve", lambda: V.scalar_tensor_tensor(out=ucb_p, in0=Upp[:, 2:2050], scalar=cwt_p[:, 2:3], in1=ucf_p, op0=ALU.mult, op1=ALU.add),
                          reads=[bUpp, bcwt_p, bucf_p], writes=[bucb_p])
                    for j0 in (0, 8):
                        p, bp = ps()
                        pv = p.bitcast(BF16)
                        for j in range(8):
                            fw.op("pe", lambda pv=pv, j=j, j0=j0: P.transpose(out=pv[:, j * 128:(j + 1) * 128], in_=ucb_p[:, (j0 + j) * 128:(j0 + j + 1) * 128], identity=ident),
                                  reads=[bucb_p, b_const], writes=[bp])
                        t1a = pc * 16 + j0
                        fw.op("act", lambda pv=pv, t1a=t1a: A.copy(out=stT[:, :, t1a:t1a + 8], in_=pv.rearrange("p (t c) -> p c t", c=128)), reads=[bp], writes=[bstT])

                for si, rbase in enumerate((0, 1024)):
                    for chk in range(4):
                        row0 = rbase + chk * 128
                        fw.dma("sp", lambda row0=row0: nc.sync.dma_start(out=cwt_p, in_=cw_all[row0:row0 + 128, :]), writes=[bcwt_p])
                        for pc in range(L // 2048):
                            pre_piece(row0, pc)
                        fw.dma("act", lambda si=si, chk=chk: nc.scalar.dma_start(out=zd[si, :, chk * 128:(chk + 1) * 128, :], in_=stT), reads=[bstT], writes=[bzd])
                fw.barrier()
                ar.reset(m0)

            bhc = Buf()
            Gf = ar.alloc(256)
            S2m = ar.alloc(12 * 128).rearrange("p (a b) -> p a b", a=12)
            INV = ar.alloc(4 * NI).rearrange("p (a b) -> p a b", a=4)
            S1I = ar.alloc(4 * 128).rearrange("p (a b) -> p a b", a=4)
            INVo = INV
            if not own_static:
                INVo = ar.alloc(4 * NIo).rearrange("p (a b) -> p a b", a=4)
                fw.dma("pool", lambda: G.dma_start(out=INVo, in_=C["INVo"]), writes=[bhc])
            if not own_static:
                Btab = ar.alloc(2 * 512, F32).rearrange("p (a c) -> p a c", a=2)
                fw.dma("sp", lambda: nc.sync.dma_start(out=Btab, in_=C["Btab"]), writes=[bhc])
            bn_t = ar.alloc(CP, F32)
            bbn = Buf()
            if own_static:
                negt = ar.alloc(2 * R, F32)
                absd = ar.alloc(512, F32)
            hbias = ar.alloc(1024, F32)
            ones_f = ar.alloc(128, F32)
            fw.dma("pool", lambda: G.dma_start(out=Gf, in_=C["Gf"]), writes=[bhc])
            fw.dma("pool", lambda: G.dma_start(out=S2m, in_=C["S2"]), writes=[bhc])
            fw.dma("pool", lambda: G.dma_start(out=INV, in_=C["INV"]), writes=[bhc])
            fw.dma("pool", lambda: G.dma_start(out=S1I, in_=C["S1I"]), writes=[bhc])
            if own_static:
                fw.dma("sp", lambda: nc.sync.dma_start(out=negt, in_=C["negt"].rearrange("p a b -> p (a b)")), writes=[bhc])
                fw.dma("sp", lambda: nc.sync.dma_start(out=absd, in_=absd_bc), writes=[bhc])
            fw.dma("sp", lambda: nc.sync.dma_start(out=hbias, in_=hb_bc), writes=[bhc])
            fw.op("dve", lambda: V.memset(ones_f, 1.0), writes=[bhc])
            fw.barrier()
            zb = ar.alloc(R * CP).rearrange("p (c t) -> p c t", c=CP)
            x1b = ar.alloc(R * CP).rearrange("p (c t) -> p c t", c=CP)
            x2b = ar.alloc(16 * CP).rearrange("p (c t) -> p c t", c=CP)
            bzb, bx1b, bx2b = Buf(), Buf(), Buf()
            kf = [[ar.alloc(R * CP).rearrange("p (c t) -> p c t", c=CP) for _ in range(2)] for _ in range(2)]
            bkf = Buf()
            win = [ar.alloc(R * CP).rearrange("p (c t) -> p c t", c=CP) for _ in range(2)]
            bwin = Buf()
            w3s = ar.alloc(4 * CP).rearrange("p (d o c) -> p d o c", d=2, o=2)
            bw3 = Buf()
            if own_static:
                h2p = [ar.alloc(2048)] * 2
                bh2p = [Buf()] * 2
            else:
                h2p = [ar.alloc(2048) for _ in range(2)]
                bh2p = [Buf(), Buf()]
            acc = ar.alloc(2 * CP, F32)
            rn = ar.alloc(2 * CP, F32)
            bacc, brn = Buf(), Buf()
            Kt = ar.alloc(4 * GP * 128).rearrange("p (h r g k) -> p h r g k", h=2, r=2, g=GP)
            bK = Buf()
            Bsb = [ar.alloc(GP * 256).rearrange("p (g k) -> p g k", g=GP) for _ in range(2)]
            bBsb = [Buf(), Buf()]
            ZsR = Rot(2, lambda: ar.alloc(4 * 512).rearrange("p (h r n) -> p h r n", h=2, r=2))
            YsR = Rot(2, lambda: ar.alloc(4 * 512).rearrange("p (h r n) -> p h r n", h=2, r=2))
            tYR = Rot(4, lambda: ar.alloc(512))
            Wsb = ar.alloc(GP * NI).rearrange("p (g n) -> p g n", g=GP)
            bWsb = Buf()
            U = [ar.alloc(1026)] * 2
            bU = [Buf()] * 2
            ucf = ar.alloc(1024, F32)
            ucb = ar.alloc(1024)
            bucf, bucb = Buf(), Buf()
            cwt = ar.alloc(4, F32)
            bcwt = Buf()
            tAR = Rot(2, lambda: ar.alloc(512, F32))
            tBR = Rot(2, lambda: ar.alloc(512, F32))
            yst = ar.alloc(16 * 128).rearrange("p (t c) -> p t c", c=128)
            byst = Buf()
            ucount = [0]

            def load_stream(row0, tok0, ntok, dst, bdst, src=None, crow=None):
                src = uT_d if src is None else src
                crow = row0 if crow is None else crow
                fw.dma("sp", lambda: nc.sync.dma_start(out=cwt[0:CP, :], in_=cw_all[crow:crow + CP, :]), writes=[bcwt])
                for pc in range(ntok // 1024):
                    u_, bu_ = U[ucount[0] % 2], bU[ucount[0] % 2]
                    ucount[0] += 1
                    s0 = tok0 + pc * 1024
                    fw.dma("sp", lambda u_=u_, s0=s0: nc.sync.dma_start(out=u_[0:CP, :], in_=src[row0:row0 + CP, s0:s0 + 1026]), reads=[buT], writes=[bu_])
                    fw.op("act", lambda u_=u_: A.activation(out=ucf[0:CP, :], in_=u_[0:CP, 1:1025], func=AF.Identity, bias=cwt[0:CP, 3:4], scale=cwt[0:CP, 1:2]),
                          reads=[bu_, bcwt], writes=[bucf])
                    fw.op("dve", lambda u_=u_: V.scalar_tensor_tensor(out=ucf[0:CP, :], in0=u_[0:CP, 0:1024], scalar=cwt[0:CP, 0:1], in1=ucf[0:CP, :],
                                                                     op0=ALU.mult, op1=ALU.add), reads=[bu_, bcwt, bucf], writes=[bucf])
                    fw.op("dve", lambda u_=u_: V.scalar_tensor_tensor(out=ucb[0:CP, :], in0=u_[0:CP, 2:1026], scalar=cwt[0:CP, 2:3], in1=ucf[0:CP, :],
                                                                     op0=ALU.mult, op1=ALU.add), reads=[bu_, bcwt, bucf], writes=[bucb])
                    per = min(8, 1024 // CP)
                    for j0 in range(0, 8, per):
                        p, bp = ps()
                        pv = p.bitcast(BF16)
                        for j in range(per):
                            fw.op("pe", lambda pv=pv, j=j, j0=j0: P.transpose(out=pv[:, j * CP:(j + 1) * CP], in_=ucb[0:CP, (j0 + j) * 128:(j0 + j + 1) * 128],
                                                                             identity=ident[0:CP, 0:CP]), reads=[bucb, b_const], writes=[bp])
                        t1a = pc * 8 + j0
                        fw.op("act", lambda pv=pv, t1a=t1a, per=per: A.copy(out=dst[:, :, t1a:t1a + per], in_=pv[:, 0:per * CP].rearrange("p (t c) -> p c t", c=CP)),
                              reads=[bp], writes=[bdst])

            def s1_stage(src, bsrc, Bd, bBd):
                for g in range(0, GP, 2):
                    p, bp = ps()
                    for gg in range(2):
                        lt = src[:, (g + gg) * cg:(g + gg + 1) * cg, :].rearrange("p c t -> p (c t)")
                        fw.op("pe", lambda p=p, gg=gg, lt=lt: P.matmul(p[:, gg * 256:(gg + 1) * 256], lhsT=lt, rhs=Gf, start=True, stop=True),
                              reads=[bsrc, bhc], writes=[bp])
                    fw.op("act", lambda p=p, g=g: A.copy(out=Bd[:, g:g + 2, :], in_=p.rearrange("p (g k) -> p g k", g=2)), reads=[bp], writes=[bBd])

            def s2_stage(g0, srcs):
                outs = {}
                for h in range(2):
                    for ri in range(2):
                        p, bp = ps()
                        nmm = 2 * len(srcs)
                        i = 0
                        for (Bd, bBd, neg) in srcs:
                            base = h * 6 + (3 if neg else 0)
                            BR = Bd[:, g0:g0 + 4, 0:128]
                            BI = Bd[:, g0:g0 + 4, 128:256]
                            pairs = ((base + 0, BR), (base + 2, BI)) if ri == 0 else ((base + 0, BI), (base + 1, BR))
                            for (mi, mv) in pairs:
                                fw.op("pe", lambda p=p, mi=mi, mv=mv, i=i, nmm=nmm: P.matmul(p, lhsT=S2m[:, mi, :], rhs=mv, start=(i == 0), stop=(i == nmm - 1)),
                                      reads=[bBd, bhc], writes=[bp])
                                i += 1
                        outs[(h, ri)] = (p, bp)
                return outs

            for pi in range(NPASS):
                c0 = pi * CP
                if own_static:
                    load_stream(1024 + c0, 0, L, zb, bzb)
                    load_stream(c0, 0, L, x1b, bx1b)
                else:
                    fw.dma("sp", lambda c0=c0: nc.sync.dma_start(out=zb, in_=zd[1, :, c0:c0 + CP, :]), reads=[bzd], writes=[bzb])
                    fw.dma("sp", lambda c0=c0: nc.sync.dma_start(out=x1b, in_=zd[0, :, c0:c0 + CP, :]), reads=[bzd], writes=[bx1b])
                for d_ in range(2):
                    for o_ in range(2):
                        cc_ = o_ * 1024 + d_ * 512 + c0
                        fw.dma("pool", lambda d_=d_, o_=o_, cc_=cc_: G.dma_start(out=w3s[0:64, d_, o_, :], in_=fw3[:, cc_:cc_ + CP]), writes=[bw3])
                for pn in range(2):
                    if own_static:
                        for l1 in range(R):
                            fw.op("act", lambda pn=pn, l1=l1, c0=c0: A.activation(out=win[pn][:, :, l1], in_=absd[:, c0:c0 + CP], func=AF.Exp,
                                                                                 scale=negt[:, pn * R + l1:pn * R + l1 + 1]), reads=[bhc], writes=[bwin])
                    else:
                        fw.dma("pool", lambda pn=pn, c0=c0: G.dma_start(out=win[pn].rearrange("p c t -> p (c t)").unsqueeze(1),
                                                                       in_=C["Atab"][pn:pn + 1, c0 * R:(c0 + CP) * R].partition_broadcast(128)),
                               writes=[bwin])
                        fw.op("pool", lambda pn=pn, c0=c0: G.tensor_tensor(out=win[pn], in0=win[pn],
                                                                          in1=Btab[:, pn, c0:c0 + CP].unsqueeze(2).to_broadcast([128, CP, R]), op=ALU.mult),
                              reads=[bwin, bhc], writes=[bwin])
                hcount = 0
                nl = max(1, 512 // (2 * CP))
                for pn in range(2):
                    for pc in range(L // 2048):
                        hp, bhp = h2p[hcount % 2], bh2p[hcount % 2]
                        hcount += 1
                        fw.dma("sp", lambda hp=hp, pn=pn, pc=pc: nc.sync.dma_start(out=hp[0:64, :], in_=h2_d[pn, :, pc * 2048:(pc + 1) * 2048]), reads=[bh2d], writes=[bhp])
                        for la in range(0, 16, nl):
                            p, bp = ps()
                            for j in range(nl):
                                fw.op("pe", lambda p=p, j=j, la=la, hp=hp, pn=pn: P.matmul(p[:, j * 2 * CP:(j + 1) * 2 * CP], lhsT=hp[0:64, (la + j) * 128:(la + j + 1) * 128],
                                                                                       rhs=w3s[0:64, pn].rearrange("p o c -> p (o c)"), start=True, stop=True),
                                      reads=[bhp, bw3], writes=[bp])
                            l1a = pc * 16 + la
                            for o in range(2):
                                fw.op("dve", lambda p=p, o=o, pn=pn, l1a=l1a: V.scalar_tensor_tensor(
                                    out=kf[o][pn][:, :, l1a:l1a + nl], in0=win[pn][:, :, l1a:l1a + nl], scalar=0.05,
                                    in1=p[:, 0:nl * 2 * CP].rearrange("p (l o c) -> p o c l", o=2, c=CP)[:, o], op0=ALU.add, op1=ALU.mult),
                                    reads=[bp, bwin], writes=[bkf])
                for o in range(2):
                    fw.op("dve", lambda o=o: V.memset(kf[o][1][0:1, :, 0:1], 0.0), writes=[bkf])
                for o in range(2):
                    for pn in range(2):
                        s1_stage(kf[o][pn], bkf, Bsb[pn], bBsb[pn])
                    for g0 in range(0, GP, 4):
                        outs = s2_stage(g0, [(Bsb[0], bBsb[0], False), (Bsb[1], bBsb[1], True)])
                        for (h, ri), (p, bp) in outs.items():
                            fw.op("act", lambda p=p, h=h, ri=ri, g0=g0: A.copy(out=Kt[:, h, ri, g0:g0 + 4, :], in_=p.rearrange("p (g k) -> p g k", g=4)),
                                  reads=[bp], writes=[bK])
                    if o == 0:
                        for o2 in range(2):
                            for pn in range(2):
                                dsta = (acc if pn == 0 else rn)[:, o2 * CP:(o2 + 1) * CP]
                                fw.op("dve", lambda o2=o2, pn=pn, dsta=dsta: V.tensor_reduce(out=dsta, in_=kf[o2][pn], axis=AX.X, op=ALU.add,
                                                                                         apply_absolute_value=True), reads=[bkf], writes=[bacc])
                        fw.op("dve", lambda: V.tensor_tensor(out=acc, in0=acc, in1=rn, op=ALU.add), reads=[bacc], writes=[bacc])
                        p, bp = ps()
                        fw.op("pe", lambda p=p: P.matmul(p[:, 0:2 * CP], lhsT=ones_f, rhs=acc, start=True, stop=True), reads=[bacc, bhc], writes=[bp])
                        fw.op("dve", lambda p=p: V.reciprocal(out=rn, in_=p[:, 0:2 * CP]), reads=[bp], writes=[brn])
                        if not own_static:
                            fw.op("dve", lambda p=p, c0=c0: V.tensor_tensor(out=bn_t, in0=p[:, CP:2 * CP], in1=hbias[:, 512 + c0:512 + c0 + CP], op=ALU.mult),
                                  reads=[bp, bhc], writes=[bbn])

                        if own_static:
                            load_stream(512 + c0, 0, 2048, x2b, bx2b)
                        else:
                            load_stream(c0, 127, 2048, x2b, bx2b, src=u2_d, crow=512 + c0)
                    if (not own_static) and o == 1:
                        for h in range(2):
                            fw.op("dve", lambda h=h: V.tensor_tensor(out=Kt[:, h, 0], in0=Kt[:, h, 0], in1=bn_t.unsqueeze(2).to_broadcast([128, GP, 128]), op=ALU.add),
                                  reads=[bK, bbn], writes=[bK])
                    s1_stage(zb, bzb, Bsb[0], bBsb[0])
                    IVm, NIc = (INV, NI) if (o == 0 or own_static) else (INVo, NIo)

                    def conv_a(g0):
                        outs = s2_stage(g0, [(Bsb[0], bBsb[0], False)])
                        Zs, bZs = ZsR.next()
                        Ys, bYs = YsR.next()
                        for (h, ri), (p, bp) in outs.items():
                            fw.op("act", lambda p=p, h=h, ri=ri, Zs=Zs: A.copy(out=Zs[:, h, ri, :], in_=p), reads=[bp], writes=[bZs])
                        for h in range(2):
                            KR = Kt[:, h, 0, g0:g0 + 4, :].rearrange("p g k -> p (g k)")
                            KI = Kt[:, h, 1, g0:g0 + 4, :].rearrange("p g k -> p (g k)")
                            ZR = Zs[:, h, 0, :]
                            ZI = Zs[:, h, 1, :]
                            ta_, bta_ = tYR.next()
                            tb_, btb_ = tYR.next()
                            fw.op("dve", lambda ZR=ZR, KR=KR, ta_=ta_: V.tensor_tensor(out=ta_, in0=ZR, in1=KR, op=ALU.mult), reads=[bZs, bK], writes=[bta_])
                            fw.op("pool", lambda ZI=ZI, KI=KI, tb_=tb_: G.tensor_tensor(out=tb_, in0=ZI, in1=KI, op=ALU.mult), reads=[bZs, bK], writes=[btb_])
                            fw.op("dve", lambda h=h, Ys=Ys, ta_=ta_, tb_=tb_: V.tensor_tensor(out=Ys[:, h, 0, :], in0=ta_, in1=tb_, op=ALU.subtract), reads=[bta_, btb_], writes=[bYs])
                            tc_, btc_ = tYR.next()
                            td_, btd_ = tYR.next()
                            fw.op("dve", lambda ZR=ZR, KI=KI, tc_=tc_: V.tensor_tensor(out=tc_, in0=ZR, in1=KI, op=ALU.mult), reads=[bZs, bK], writes=[btc_])
                            fw.op("pool", lambda ZI=ZI, KR=KR, td_=td_: G.tensor_tensor(out=td_, in0=ZI, in1=KR, op=ALU.mult), reads=[bZs, bK], writes=[btd_])
                            fw.op("dve", lambda h=h, Ys=Ys, tc_=tc_, td_=td_: V.tensor_tensor(out=Ys[:, h, 1, :], in0=tc_, in1=td_, op=ALU.add), reads=[btc_, btd_], writes=[bYs])
                        return Ys, bYs

                    def conv_b(g0, Ys, bYs):
                        for gl in range(4):
                            p, bp = ps()
                            i = 0
                            for h in range(2):
                                for ri in range(2):
                                    fw.op("pe", lambda p=p, h=h, ri=ri, gl=gl, i=i, NIc=NIc, IVm=IVm: P.matmul(p[:, 0:NIc], lhsT=Ys[:, h, ri, gl * 128:(gl + 1) * 128], rhs=IVm[:, 2 * h + ri, :],
                                                                                           start=(i == 0), stop=(i == 3)), reads=[bYs, bhc], writes=[bp])
                                    i += 1
                            fw.op("act", lambda p=p, gl=gl, NIc=NIc: A.copy(out=Wsb[:, g0 + gl, 0:NIc], in_=p[:, 0:NIc]), reads=[bp], writes=[bWsb])

                    prev = None
                    for g0 in list(range(0, GP, 4)) + [None]:
                        cur = None
                        if g0 is not None:
                            cur = (g0,) + conv_a(g0)
                        if prev is not None:
                            conv_b(*prev)
                        prev = cur
                    folded = (not own_static) and o == 1
                    if folded:
                        Wv = Wsb[:, :, 0:NIo].rearrange("p g (r c t) -> p g r c t", r=2, c=cg)
                    else:
                        Wv = Wsb.rearrange("p g (r c t) -> p g r c t", r=2, c=cg)
                    NT1b = min(NT1, 16)
                    blocks = range(0, R, NT1) if o == 0 else range(0, 16, NT1b)
                    for t1a in blocks:
                        nt = NT1 if o == 0 else NT1b
                        ncol = nt * CP
                        p, bp = ps()
                        for i, (mi, ri, sh) in enumerate(((0, 0, 1), (1, 1, 1), (2, 0, 0), (3, 1, 0))):
                            mv = Wv[:, :, ri, :, t1a + sh:t1a + sh + nt]
                            fw.op("pe", lambda p=p, mi=mi, mv=mv, i=i, ncol=ncol: P.matmul(p[:, 0:ncol], lhsT=S1I[:, mi, :], rhs=mv, start=(i == 0), stop=(i == 3)),
                                  reads=[bWsb, bhc], writes=[bp])
                        cv = p[:, 0:ncol].rearrange("p (c t) -> p c t", t=nt)
                        tA, btA = tAR.next()
                        tB, btB = tBR.next()
                        tAf = tA[:, 0:ncol]
                        tBf = tB[:, 0:ncol]
                        tAv = tAf.rearrange("p (c t) -> p c t", c=CP)
                        tBv = tBf.rearrange("p (c t) -> p c t", c=CP)
                        rnb = rn[:, o * CP:(o + 1) * CP].unsqueeze(2).to_broadcast([128, CP, nt])
                        hbb = hbias[:, o * 512 + c0:o * 512 + c0 + CP].unsqueeze(2).to_broadcast([128, CP, nt])
                        fw.op("dve", lambda cv=cv, tAv=tAv, rnb=rnb: V.tensor_tensor(out=tAv, in0=cv, in1=rnb, op=ALU.mult), reads=[bp, brn], writes=[btA])
                        if not folded:
                            zin = zb[:, :, t1a:t1a + nt]
                            fw.op("pool", lambda zin=zin, tBv=tBv, hbb=hbb: G.tensor_tensor(out=tBv, in0=zin, in1=hbb, op=ALU.mult), reads=[bzb, bhc], writes=[btB])
                            fw.op("dve", lambda tAf=tAf, tBf=tBf: V.tensor_tensor(out=tAf, in0=tAf, in1=tBf, op=ALU.add), reads=[btA, btB], writes=[btA])
                        if o == 0:
                            gate = x1b[:, :, t1a:t1a + nt]
                            fw.op("dve", lambda zin=zin, tAv=tAv, gate=gate: V.tensor_tensor(out=zin, in0=tAv, in1=gate, op=ALU.mult),
                                  reads=[btA, bx1b], writes=[bzb])
                        else:
                            tl = t1a
                            gate = x2b[:, :, tl:tl + nt]
                            co = c0 % 128
                            fw.op("dve", lambda tl=tl, tAv=tAv, gate=gate, co=co, nt=nt: V.tensor_tensor(out=yst[:, tl:tl + nt, co:co + CP].rearrange("p t c -> p c t"), in0=tAv, in1=gate, op=ALU.mult),
                                  reads=[btA, bx2b], writes=[byst])
                if (c0 + CP) % 128 == 0:
                    ch = c0 // 128
                    for j0 in (0, 8):
                        p, bp = ps()
                        pv = p.bitcast(BF16)
                        for j in range(8):
                            fw.op("pe", lambda pv=pv, j=j, j0=j0: P.transpose(out=pv[:, j * 128:(j + 1) * 128], in_=yst[:, j0 + j, :], identity=ident),
                                  reads=[byst, b_const], writes=[bp])
                        fw.op("act", lambda pv=pv, ch=ch, j0=j0: A.copy(out=hyT[:, ch, j0 * 128:(j0 + 8) * 128], in_=pv), reads=[bp], writes=[bhyT])
            fw.barrier()
            ar.reset(m0)

        mtop = ar.mark()
        hyT_s = ar.alloc(4 * NTOK).rearrange("p (k t) -> p k t", k=4)
        bhy_s = Buf()
        fw.op("dve", lambda: V.memset(hyT_s, 0.0), writes=[bhy_s])
        mp = ar.mark()
        hyT_p = ar.alloc(4 * NTOK).rearrange("p (k t) -> p k t", k=4)
        bhy_p = Buf()
        fw.op("dve", lambda: V.memset(hyT_p, 0.0), writes=[bhy_p])
        if KHY & 1:
            hyena_phase("p", lambda t, n=1: xp[t * 128:(t + n) * 128, :], 2048, 8, True, hyT_p, bhy_p)
        if KHY & 2:
            hyena_phase("s", lambda t, n=1: xs_all[t * 128:(t + n) * 128, :], 16384, 1, False, hyT_s, bhy_s)
        for _ in range(NWB):
            wbuf.append(ar.alloc(8 * 512))
        run_group(lambda t, n=1: xp[t * 128:(t + n) * 128, :], 16, 0, 16, ropep, yp, False, hyT_p, bhy_p)
        fw.barrier()
        ar.reset(mp)
        del wbuf[:]
        for _ in range(NWB):
            wbuf.append(ar.alloc(8 * 512))
        if KGROUPS >= 2:
            run_group(lambda t, n=1: xs[t * 128:(t + n) * 128, :], 18, 1, 16, ropes, ys, True, hyT_s, bhy_s)
        fw.barrier()
        fw.replay(block)
    return nc


def _host_consts(core):
    inv = 10000.0 ** (-np.arange(0, 64, 2, dtype=np.float32) / 64)

    def rope_tab(pos):
        ang = pos.astype(np.float32)[:, None] * inv[None, :]
        tab = np.concatenate([np.cos(ang), np.sin(ang)], axis=1).astype(np.float32)
        n = tab.shape[0] // 128
        return np.ascontiguousarray(tab.reshape(n, 128, 64).transpose(1, 0, 2))
    rp = rope_tab(np.arange(NTOK))
    rs = rope_tab(np.arange(core * NTOK - 128, (core + 1) * NTOK + 128))
    a = np.arange(128)[:, None]
    b = np.arange(128)[None, :]
    triL = (b <= a).astype(np.float32)
    triU = (a <= b).astype(np.float32)
    haloL = triL * (0.0 if core == 0 else 1.0)
    haloR = triU * (0.0 if core == NCORES - 1 else 1.0)
    masks = np.ascontiguousarray(np.stack([triL, triU, haloL, haloR], axis=1)).astype(np.float32)
    return rp, rs, masks


def _hy_consts(L, cg):
    R = L // 128
    P1 = 2 * R
    t2 = np.arange(128)
    k2 = np.arange(128)
    ang = 2 * np.pi * np.outer(t2, k2 + 0.5) / 256
    Gf = np.concatenate([np.cos(ang), -np.sin(ang)], axis=1)
    t1 = np.arange(R)
    eye = np.eye(cg)
    S2 = []
    for h in range(2):
        k1 = h * R + np.arange(R)
        a1 = 2 * np.pi * np.outer(t1, k1) / P1
        sg = ((-1.0) ** k1)[None, :]
        for sgn in (np.ones_like(sg), sg):
            S2.append(np.kron(eye, np.cos(a1) * sgn))
            S2.append(np.kron(eye, -np.sin(a1) * sgn))
            S2.append(np.kron(eye, np.sin(a1) * sgn))
    S2 = np.stack(S2, axis=1)
    t1o = np.arange(-1, R)
    NI = 2 * cg * (R + 1)
    INV = []
    for h in range(2):
        k1 = h * R + np.arange(R)
        ai = 2 * np.pi * np.outer(k1, t1o) / P1
        IR = np.kron(eye, np.cos(ai) / P1)
        II = np.kron(eye, np.sin(ai) / P1)
        INV.append(np.concatenate([IR, II], axis=1))
        INV.append(np.concatenate([-II, IR], axis=1))
    INV = np.stack(INV, axis=1)
    tt = np.arange(256)
    ag = 2 * np.pi * np.outer(k2 + 0.5, tt) / 256
    C = (2.0 / 256) * np.cos(ag)
    S = -(2.0 / 256) * np.sin(ag)
    S1I = np.stack([C[:, :128], S[:, :128], C[:, 128:], S[:, 128:]], axis=1)
    bands = np.linspace(1e-4, 15, 16, dtype=np.float32)
    def feats(pos):
        pos = pos.astype(np.float32)
        t = pos / np.float32(L - 1)
        a = (np.float32(2.0 * math.pi / L) * pos[:, None]) * bands[None, :]
        return np.concatenate([t[:, None], np.cos(a), -np.sin(a)], axis=1).astype(np.float32).T
    p = np.arange(L)
    zf = feats(p)
    zr = feats((L - p).astype(np.float64))
    negt = np.stack([-(p / (L - 1.0)), -((L - p) / (L - 1.0))], axis=0).astype(np.float32)
    negt = np.ascontiguousarray(negt.reshape(2, R, 128).transpose(2, 0, 1))
    f32 = lambda a: np.ascontiguousarray(a, dtype=np.float32)
    dl = np.abs(np.linspace(math.log(1e-2) / 1.5, math.log(1e-2) / 0.3, 512)).astype(np.float64)
    l1v = np.arange(R, dtype=np.float64)
    l2v = np.arange(128, dtype=np.float64)
    A0 = np.exp(-dl[:, None] * (128.0 * l1v[None, :]) / (L - 1.0))
    A1 = np.exp(-dl[:, None] * (L - 128.0 * l1v[None, :]) / (L - 1.0))
    B0 = np.exp(-l2v[:, None] * dl[None, :] / (L - 1.0))
    B1 = np.exp(+l2v[:, None] * dl[None, :] / (L - 1.0))
    Atab = f32(np.stack([A0.reshape(-1), A1.reshape(-1)], axis=0))
    Btab = f32(np.stack([B0, B1], axis=1))
    if cg == 1:
        return dict(Gf=f32(Gf), S2=f32(S2), INV=f32(INV), S1I=f32(S1I), zf=f32(np.stack([zf, zr], 0)), negt=negt, Atab=Atab, Btab=Btab)
    return dict(Gf=f32(Gf), S2=f32(S2), INV=f32(INV), S1I=f32(S1I), zf=f32(np.stack([zf, zr], 0)), negt=negt)


_NC_CACHE = {}


def kernel(**inputs):
    f = lambda k: np.ascontiguousarray(np.asarray(inputs[k], dtype=np.float32))
    x_prompt = f("x_prompt")
    x_sample = f("x_sample")[0]
    if "nc" not in _NC_CACHE:
        _NC_CACHE["nc"] = build_program()
    nc = _NC_CACHE["nc"]
    rep = lambda v, n=128: np.ascontiguousarray(np.broadcast_to(v.reshape(1, -1), (n, v.size)))
    common = {
        "w_in": f("w_in")[0], "w_hy": f("w_hy_out")[0], "w_at": f("w_at_out")[0], "w_o": f("w_o")[0],
        "w_gate": f("w_gate")[0], "w_up": f("w_up")[0], "w_down": f("w_down")[0],
        "g1_bc": rep(f("attn_norm_w")[0]), "g2_bc": rep(f("ffn_norm_w")[0]),
        "qw_bc": rep(np.tile(f("q_norm_w")[0], 8)), "kw_bc": rep(np.tile(f("k_norm_w")[0], 2)),
        "sink_bc": rep(f("attn_sink")[0]),
        "ident": np.eye(128, dtype=np.float32),
        "xs_all": x_sample,
        "w_in_hy": np.ascontiguousarray(f("w_in")[0][:, :1536]),
        "cw_all": np.ascontiguousarray(np.concatenate([f("hyena_conv_w")[0], f("hyena_conv_b")], axis=0).T),
        "fw1": f("filt_w1")[0], "fw2": f("filt_w2")[0], "fw3": f("filt_w3")[0],
        "fb": np.ascontiguousarray(np.stack([f("filt_b1")[0], f("filt_b2")[0], f("filt_freq")[0]], axis=1)),
        "hb_bc": rep(f("hyena_bias")[0].reshape(-1)),
        "absd_bc": rep(np.abs(np.linspace(math.log(1e-2) / 1.5, math.log(1e-2) / 0.3, 512, dtype=np.float32))),
    }
    for tag, (L_, cg_) in (("p", (2048, 8)), ("s", (16384, 1))):
        for k_, v_ in _hy_consts(L_, cg_).items():
            common[k_ + "_" + tag] = v_
    xs_pad = np.concatenate([np.zeros((128, D), np.float32), x_sample, np.zeros((128, D), np.float32)], axis=0)
    in_maps = []
    for c in range(NCORES):
        rp, rs, masks = _host_consts(c)
        m = dict(common)
        m["xp"] = x_prompt[c]
        m["xs"] = np.ascontiguousarray(xs_pad[c * NTOK:(c + 1) * NTOK + 256])
        m["rope_p"] = rp
        m["rope_s"] = rs
        m["masks"] = masks
        R_, P1_ = 128, 256
        t1o = np.arange(16 * c - 1, 16 * c + 16)
        invo = []
        for h in range(2):
            k1 = h * R_ + np.arange(R_)
            ai = 2 * np.pi * np.outer(k1, t1o) / P1_
            IR = np.cos(ai) / P1_
            II = np.sin(ai) / P1_
            invo.append(np.concatenate([IR, II], axis=1))
            invo.append(np.concatenate([-II, IR], axis=1))
        m["INVo_s"] = np.ascontiguousarray(np.stack(invo, axis=1), dtype=np.float32)
        in_maps.append(m)
    res = run_bass_kernel_spmd(nc, in_maps, core_ids=list(range(NCORES)))
    y_p = np.stack([np.asarray(res.results[c]["yp"], dtype=np.float32) for c in range(NCORES)], axis=0)
    y_s = np.concatenate([np.asarray(res.results[c]["ys"], dtype=np.float32) for c in range(NCORES)], axis=0)[None]
    return (y_p, y_s)
```
